# Optimizing a Trainium2 kernel written in Bass

```python
import jax, jax.numpy as jnp
from jax import lax
import numpy as np

D_MODEL = 2048
BATCH = 4
SEQ = 4096
DEPTH = 1

ATTN_HEADS = 16
ATTN_KV_HEADS = 2
ATTN_HEAD_DIM = 64
WINDOW = 128
ATTN_BLOCK = 128
ROPE_THETA = 500000.0
ROPE_DIM = ATTN_HEAD_DIM // 4
DN_HEADS = 8
DN_HEAD_K = 128
DN_HEAD_V = 128
DN_CHUNK = 64
CONV_K = 4
D_FF = 5632
NORM_EPS = 1e-6
L2_EPS = 1e-6
N_MOD = 9

ATTN_WIDTH = ATTN_HEADS * ATTN_HEAD_DIM
ATTN_KV_WIDTH = ATTN_KV_HEADS * ATTN_HEAD_DIM
DN_QK_WIDTH = DN_HEADS * DN_HEAD_K
DN_WIDTH = DN_HEADS * DN_HEAD_V
MIX_WIDTH = ATTN_WIDTH + DN_WIDTH
IN_SPLITS = (ATTN_WIDTH, ATTN_KV_WIDTH, ATTN_KV_WIDTH, DN_QK_WIDTH, DN_QK_WIDTH, DN_WIDTH, DN_WIDTH, DN_HEADS, DN_HEADS)
IN_WIDTH = sum(IN_SPLITS)
IN_OFFSETS = tuple(int(o) for o in np.cumsum(IN_SPLITS)[:-1])
CONV_CH = 2 * DN_QK_WIDTH + DN_WIDTH

kernel_name = "hymba_swa_sink_gdn_macaron_adaln"


def rms_norm(t, w):
    tf = t.astype(jnp.float32)
    y = tf * lax.rsqrt(jnp.mean(tf * tf, axis=-1, keepdims=True) + NORM_EPS) * w.astype(jnp.float32)
    return y.astype(t.dtype)


def modulate(t, shift, scale):
    return t * (1 + scale[:, None, :]) + shift[:, None, :]


def swiglu(u, w_gate, w_up, w_down):
    return (jax.nn.silu(u @ w_gate) * (u @ w_up)) @ w_down


def rope_tables(positions):
    inv_freq = ROPE_THETA ** (-jnp.arange(0, ROPE_DIM, 2, dtype=jnp.float32) / ROPE_DIM)
    ang = positions.astype(jnp.float32)[..., None] * inv_freq
    return jnp.cos(ang), jnp.sin(ang)


def apply_partial_rope(t, cos, sin):
    half = ROPE_DIM // 2
    cos = cos[:, :, None, :]
    sin = sin[:, :, None, :]
    t1 = t[..., :half]
    t2 = t[..., half:ROPE_DIM]
    return jnp.concatenate([t1 * cos - t2 * sin, t2 * cos + t1 * sin, t[..., ROPE_DIM:]], axis=-1)


def sliding_window_sink_attention(q, k, v, cos, sin, sinks):
    B, S, _ = q.shape
    nb = S // ATTN_BLOCK
    G = ATTN_HEADS // ATTN_KV_HEADS
    f32 = jnp.float32
    q = apply_partial_rope(q.astype(f32).reshape(B, S, ATTN_HEADS, ATTN_HEAD_DIM), cos, sin)
    k = apply_partial_rope(k.astype(f32).reshape(B, S, ATTN_KV_HEADS, ATTN_HEAD_DIM), cos, sin)
    v = v.astype(f32).reshape(B, S, ATTN_KV_HEADS, ATTN_HEAD_DIM)
    qb = q.reshape(B, nb, ATTN_BLOCK, ATTN_KV_HEADS, G, ATTN_HEAD_DIM)

    def band(t):
        tb = t.reshape(B, nb, ATTN_BLOCK, ATTN_KV_HEADS, ATTN_HEAD_DIM)
        prev = jnp.pad(tb, ((0, 0), (1, 0), (0, 0), (0, 0), (0, 0)))[:, :-1]
        return jnp.concatenate([prev, tb], axis=2)

    kb, vb = band(k), band(v)
    blk = jnp.arange(nb)[:, None, None] * ATTN_BLOCK
    q_pos = blk + jnp.arange(ATTN_BLOCK)[None, :, None]
    k_pos = blk - ATTN_BLOCK + jnp.arange(2 * ATTN_BLOCK)[None, None, :]
    valid = (k_pos <= q_pos) & (k_pos > q_pos - WINDOW) & (k_pos >= 0)
    scores = jnp.einsum('bnqhgd,bnkhd->bnhgqk', qb, kb) * (ATTN_HEAD_DIM ** -0.5)
    scores = jnp.where(valid[None, :, None, None], scores, -jnp.inf)
    sink = sinks.astype(f32).reshape(1, 1, ATTN_KV_HEADS, G, 1, 1)
    m = jnp.maximum(scores.max(axis=-1, keepdims=True), sink)
    p = jnp.exp(scores - m)
    probs = p / (p.sum(axis=-1, keepdims=True) + jnp.exp(sink - m))
    out = jnp.einsum('bnhgqk,bnkhd->bnqhgd', probs, vb)
    return out.reshape(B, S, ATTN_WIDTH)


def causal_short_conv(z, w):
    y = lax.conv_general_dilated(z, w[:, None, :].astype(z.dtype), window_strides=(1,),
                                 padding=[(CONV_K - 1, 0)],
                                 dimension_numbers=('NWC', 'WIO', 'NWC'),
                                 feature_group_count=z.shape[-1])
    return jax.nn.silu(y)


def l2_normalize(t):
    return t * lax.rsqrt(jnp.sum(t * t, axis=-1, keepdims=True) + L2_EPS)


def chunk_gated_delta_rule(q, k, v, g, beta):
    B, S, H, dk = q.shape
    dv = v.shape[-1]
    C = DN_CHUNK
    n = S // C

    def chunks(t):
        return t.reshape(B, n, C, H, -1).transpose(0, 3, 1, 2, 4)

    q, k, v = chunks(q), chunks(k), chunks(v)
    beta = beta.reshape(B, n, C, H).transpose(0, 3, 1, 2)
    g = jnp.cumsum(g.reshape(B, n, C, H).transpose(0, 3, 1, 2), axis=-1)
    tri = jnp.tril(jnp.ones((C, C), dtype=bool))
    strict = jnp.tril(jnp.ones((C, C), dtype=bool), -1)
    decay = jnp.exp(jnp.where(tri, g[..., :, None] - g[..., None, :], -jnp.inf))
    k_beta = k * beta[..., None]
    v_beta = v * beta[..., None]
    A = jnp.where(strict, jnp.einsum('bhncd,bhnsd->bhncs', k_beta, k) * decay, 0.0)
    eye = jnp.eye(C, dtype=q.dtype)
    T = lax.linalg.triangular_solve(A + eye, jnp.broadcast_to(eye, A.shape), left_side=True,
                                    lower=True, unit_diagonal=True)
    u = jnp.einsum('bhncs,bhnsd->bhncd', T, v_beta)
    w = jnp.einsum('bhncs,bhnsd->bhncd', T, k_beta * jnp.exp(g)[..., None])
    a_intra = jnp.where(tri, jnp.einsum('bhncd,bhnsd->bhncs', q, k) * decay, 0.0)
    q_g = q * jnp.exp(g)[..., None]
    g_last = g[..., -1]
    k_dec = k * jnp.exp(g_last[..., None] - g)[..., None]

    def step(state, inp):
        u_i, w_i, qg_i, a_i, kd_i, gl_i = inp
        v_new = u_i - jnp.einsum('bhck,bhkv->bhcv', w_i, state)
        o_i = jnp.einsum('bhck,bhkv->bhcv', qg_i, state) + jnp.einsum('bhcs,bhsv->bhcv', a_i, v_new)
        state = state * jnp.exp(gl_i)[..., None, None] + jnp.einsum('bhck,bhcv->bhkv', kd_i, v_new)
        return state, o_i

    xs = (jnp.moveaxis(u, 2, 0), jnp.moveaxis(w, 2, 0), jnp.moveaxis(q_g, 2, 0),
          jnp.moveaxis(a_intra, 2, 0), jnp.moveaxis(k_dec, 2, 0), jnp.moveaxis(g_last, 2, 0))
    state0 = jnp.zeros((B, H, dk, dv), dtype=q.dtype)
    _, o = lax.scan(step, state0, xs)
    return o.transpose(1, 0, 3, 2, 4).reshape(B, S, H, dv)


def gated_deltanet(q, k, v, gate, a, b, conv_w, a_log, dt_bias, norm_w):
    B, S, _ = q.shape
    f32 = jnp.float32
    qkv = causal_short_conv(jnp.concatenate([q, k, v], axis=-1), conv_w).astype(f32)
    q, k, v = jnp.split(qkv, [DN_QK_WIDTH, 2 * DN_QK_WIDTH], axis=-1)
    q = l2_normalize(q.reshape(B, S, DN_HEADS, DN_HEAD_K)) * (DN_HEAD_K ** -0.5)
    k = l2_normalize(k.reshape(B, S, DN_HEADS, DN_HEAD_K))
    v = v.reshape(B, S, DN_HEADS, DN_HEAD_V)
    beta = jax.nn.sigmoid(b.astype(f32))
    g = -jnp.exp(a_log.astype(f32)) * jax.nn.softplus(a.astype(f32) + dt_bias.astype(f32))
    o = chunk_gated_delta_rule(q, k, v, g, beta)
    o = rms_norm(o, norm_w) * jax.nn.silu(gate.astype(f32).reshape(B, S, DN_HEADS, DN_HEAD_V))
    return o.reshape(B, S, DN_WIDTH)


def hybrid_mixer(u, cos, sin, w_in, conv_w, a_log, dt_bias, attn_sinks, dn_norm_w, w_out):
    proj = u @ w_in
    aq, ak, av, dq, dk, dv, dg, da, db = jnp.split(proj, IN_OFFSETS, axis=-1)
    attn_out = sliding_window_sink_attention(aq, ak, av, cos, sin, attn_sinks)
    dn_out = gated_deltanet(dq, dk, dv, dg, da, db, conv_w, a_log, dt_bias, dn_norm_w)
    mixed = jnp.concatenate([attn_out, dn_out], axis=-1).astype(u.dtype)
    return mixed @ w_out


def setup_inputs(seed: int = 0) -> dict:
    key = jax.random.key(seed)
    ks = jax.random.split(key, 24)
    f32 = jnp.float32
    L, D = DEPTH, D_MODEL

    def dense(k, shape, fan_in):
        return jax.random.normal(k, shape, f32) * fan_in ** -0.5

    def gain(k, shape):
        return 1.0 + 0.02 * jax.random.normal(k, shape, f32)

    dt = jnp.exp(jax.random.uniform(ks[12], (L, DN_HEADS), f32, minval=np.log(0.001), maxval=np.log(0.1)))
    return {
        "x": jax.random.normal(ks[0], (BATCH, SEQ, D), f32),
        "c": jax.random.normal(ks[1], (BATCH, D), f32),
        "positions": jnp.tile(jnp.arange(SEQ, dtype=jnp.int32)[None, :], (BATCH, 1)),
        "ada_w": dense(ks[2], (L, D, N_MOD * D), D),
        "ada_b": 0.02 * jax.random.normal(ks[3], (L, N_MOD * D), f32),
        "norm_ffn1": gain(ks[4], (L, D)),
        "ffn1_w_gate": dense(ks[5], (L, D, D_FF), D),
        "ffn1_w_up": dense(ks[6], (L, D, D_FF), D),
        "ffn1_w_down": dense(ks[7], (L, D_FF, D), D_FF),
        "norm_mix": gain(ks[8], (L, D)),
        "w_in": dense(ks[9], (L, D, IN_WIDTH), D),
        "conv_w": dense(ks[10], (L, CONV_K, CONV_CH), CONV_K),
        "a_log": jnp.log(jax.random.uniform(ks[11], (L, DN_HEADS), f32, minval=1.0, maxval=16.0)),
        "dt_bias": dt + jnp.log(-jnp.expm1(-dt)),
        "attn_sinks": 0.5 * jax.random.normal(ks[13], (L, ATTN_HEADS), f32),
        "dn_norm_w": gain(ks[14], (L, DN_HEAD_V)),
        "w_out": dense(ks[15], (L, MIX_WIDTH, D), MIX_WIDTH),
        "norm_ffn2": gain(ks[16], (L, D)),
        "ffn2_w_gate": dense(ks[17], (L, D, D_FF), D),
        "ffn2_w_up": dense(ks[18], (L, D, D_FF), D),
        "ffn2_w_down": dense(ks[19], (L, D_FF, D), D_FF),
        "final_norm": gain(ks[20], (D,)),
    }


def reference(x, c, positions, ada_w, ada_b, norm_ffn1, ffn1_w_gate, ffn1_w_up, ffn1_w_down,
              norm_mix, w_in, conv_w, a_log, dt_bias, attn_sinks, dn_norm_w, w_out,
              norm_ffn2, ffn2_w_gate, ffn2_w_up, ffn2_w_down, final_norm):
    cos, sin = rope_tables(positions)
    c_act = jax.nn.silu(c)
    h = x
    for l in range(DEPTH):
        mod = c_act @ ada_w[l] + ada_b[l]
        sh1, sc1, g1, sh2, sc2, g2, sh3, sc3, g3 = jnp.split(mod, N_MOD, axis=-1)
        u = modulate(rms_norm(h, norm_ffn1[l]), sh1, sc1)
        h = h + 0.5 * g1[:, None, :] * swiglu(u, ffn1_w_gate[l], ffn1_w_up[l], ffn1_w_down[l])
        u = modulate(rms_norm(h, norm_mix[l]), sh2, sc2)
        h = h + g2[:, None, :] * hybrid_mixer(u, cos, sin, w_in[l], conv_w[l], a_log[l], dt_bias[l],
                                              attn_sinks[l], dn_norm_w[l], w_out[l])
        u = modulate(rms_norm(h, norm_ffn2[l]), sh3, sc3)
        h = h + 0.5 * g3[:, None, :] * swiglu(u, ffn2_w_gate[l], ffn2_w_up[l], ffn2_w_down[l])
    return rms_norm(h, final_norm)
```

```python
import numpy as np
from contextlib import ExitStack
import concourse.bass as bass
import concourse.mybir as mybir
from concourse.bass_utils import run_bass_kernel_spmd

F32 = mybir.dt.float32
BF16 = mybir.dt.bfloat16
I32 = mybir.dt.int32
AF = mybir.ActivationFunctionType
ALU = mybir.AluOpType
AX = mybir.AxisListType

ENGS = ("pe", "act", "dve", "pool", "sp")

D = 2048
DFF = 5632
NS = 4096
KC = 16
FB = 256
NFB = DFF // FB
NMOD = 9
EPS = 1e-6
TWO_PI = float(2 * np.pi)
import os as _osenv
SELFSYNC = _osenv.environ.get("SELFSYNC", "1") == "1"


class Buf:
    __slots__ = ("name", "w", "r", "excl")

    def __init__(self, name="", excl=False):
        self.name = name
        self.w = None
        self.r = []
        self.excl = excl


class DmaSem:
    __slots__ = ("sem", "val")

    def __init__(self, sem):
        self.sem = sem
        self.val = 0


class Sched:
    def __init__(self, nc, sems, es):
        self.nc = nc
        self.es = es
        self.sems = sems
        self.count = {e: 0 for e in ENGS}
        self.prog = {e: [] for e in ENGS}
        self.seen = {e: {} for e in ENGS}
        self.semobj = {}
        self.dsems = []
        for e in ENGS:
            self.semobj[("e", e)] = sems[e]

    def new_dsem(self, name=None):
        d = DmaSem(self.es.enter_context(self.nc.semaphore(name or ("dq%d" % len(self.dsems)))))
        self.dsems.append(d)
        self.semobj[("d", id(d))] = d.sem
        return d

    def _need(self, eng, tok, waits):
        if tok is None:
            return
        key, val = tok
        if key == ("e", eng) and (eng == "pe" or not SELFSYNC):
            return
        if self.seen[eng].get(key, 0) >= val:
            return
        self.seen[eng][key] = val
        waits.append((key, val))

    def _deps(self, eng, reads, writes, extra):
        waits = []
        for b in reads:
            self._need(eng, b.w, waits)
        for b in writes:
            self._need(eng, b.w, waits)
            for t in b.r:
                self._need(eng, t, waits)
        for t in extra:
            self._need(eng, t, waits)
        return waits

    def _post(self, tok, reads, writes):
        for b in reads:
            b.r.append(tok)
        for b in writes:
            b.w = tok
            b.r = []

    def op(self, eng, fn, reads=(), writes=(), extra=()):
        ex = [b for b in reads if b.excl]
        if ex:
            writes = list(writes) + [b for b in ex if b not in writes]
        waits = self._deps(eng, reads, writes, extra)
        self.count[eng] += 1
        tok = (("e", eng), self.count[eng])
        self.prog[eng].append((waits, fn, None))
        self._post(tok, reads, writes)
        return tok

    def dma(self, q, out, in_, dsem, reads=(), writes=(), extra=()):
        waits = self._deps(q, reads, writes, extra)
        dsem.val += 16
        tok = (("d", id(dsem)), dsem.val)
        self.prog[q].append((waits, (lambda e: e.dma_start(out=out, in_=in_)), dsem))
        self._post(tok, reads, writes)
        return tok

    def dmas(self, q, pairs, dsem, reads=(), writes=(), extra=()):
        waits = self._deps(q, reads, writes, extra)
        for k, (o, i) in enumerate(pairs):
            dsem.val += 16
            self.prog[q].append((waits if k == 0 else [], (lambda e, o=o, i=i: e.dma_start(out=o, in_=i)), dsem))
        tok = (("d", id(dsem)), dsem.val)
        self._post(tok, reads, writes)
        return tok

    def barrier(self):
        toks = [(("e", e), self.count[e]) for e in ENGS if self.count[e] > 0]
        toks += [(("d", id(d)), d.val) for d in self.dsems if d.val > 0]
        for e in ENGS:
            waits = []
            for t in toks:
                if t[0] == ("e", e):
                    continue
                self._need(e, t, waits)
            self.prog[e].append((waits, None, None))

    def emit(self, block):
        me = self
        progs = self.prog
        self.prog = {e: [] for e in ENGS}

        def run(name, engine):
            for waits, fn, dsem in progs[name]:
                for key, val in waits:
                    engine.wait_ge(me.semobj[key], val)
                if fn is None:
                    continue
                inst = fn(engine)
                if dsem is not None:
                    inst.then_inc(dsem.sem, 16)
                else:
                    inst.then_inc(me.sems[name], 1)

        @block.tensor
        def _(e):
            run("pe", e)

        @block.scalar
        def _(e):
            run("act", e)

        @block.vector
        def _(e):
            run("dve", e)

        @block.gpsimd
        def _(e):
            run("pool", e)

        @block.sync
        def _(e):
            run("sp", e)

    def mm(self, out, lhsT, rhs, start=True, stop=True, reads=(), writes=()):
        return self.op("pe", lambda e: e.matmul(out, lhsT=lhsT, rhs=rhs, start=start, stop=stop), reads, writes)

    def tr(self, out, in_, ident, reads=(), writes=()):
        return self.op("pe", lambda e: e.transpose(out, in_, ident), reads, writes)

    def act(self, out, in_, func, bias=None, scale=None, accum=None, reads=(), writes=()):
        kw = {}
        if bias is not None:
            kw["bias"] = bias
        if scale is not None:
            kw["scale"] = scale
        if accum is not None:
            kw["accum_out"] = accum
        return self.op("act", lambda e: e.activation(out=out, in_=in_, func=func, **kw), reads, writes)

    def copy(self, eng, out, in_, reads=(), writes=()):
        if eng == "act":
            return self.op("act", lambda e: e.activation(out=out, in_=in_, func=AF.Identity), reads, writes)
        return self.op(eng, lambda e: e.tensor_copy(out=out, in_=in_), reads, writes)

    def tt(self, eng, out, in0, in1, op, reads=(), writes=()):
        return self.op(eng, lambda e: e.tensor_tensor(out=out, in0=in0, in1=in1, op=op), reads, writes)

    def ts(self, eng, out, in0, s1, op0, s2=None, op1=None, accum=None, reads=(), writes=()):
        kw = {}
        if op1 is not None:
            kw["op1"] = op1
        if accum is not None:
            kw["accum_out"] = accum
        return self.op(eng, lambda e: e.tensor_scalar(out=out, in0=in0, scalar1=s1, scalar2=s2, op0=op0, **kw), reads, writes)

    def stt(self, eng, out, in0, scalar, in1, op0, op1, reads=(), writes=()):
        return self.op(eng, lambda e: e.scalar_tensor_tensor(out=out, in0=in0, scalar=scalar, in1=in1, op0=op0, op1=op1),
                       reads, writes)

    def memset(self, eng, ap, val, writes=()):
        return self.op(eng, lambda e: e.memset(ap, val), (), writes)


def build_program(cfg):
    NT = cfg["NT"]
    TG = cfg["TG"]
    NG = cfg["NG"]
    TOK0 = cfg.get("TOK0", 0)
    skip_mixer = cfg.get("skip_mixer", False)
    dbg = cfg.get("dbg", ())
    NGRP = 0 if cfg.get("mixer_test") else NT // TG
    NTT = TG // 512
    NSUB = TG // 128

    nc = bass.Bass("TRN2", target_bir_lowering=False)

    def din(name, shape, dt=F32):
        return nc.dram_tensor(name, list(shape), dt, kind="ExternalInput").ap()

    def dint(name, shape, dt=F32):
        return nc.dram_tensor(name, list(shape), dt, kind="Internal").ap()

    x_d = din("x", [NT, D])
    ccol_d = din("ccol", [128, KC])
    adaw_d = din("ada_w", [D, NMOD * D])
    adab_d = din("adab_col", [128, NMOD * KC])
    ncol_d = din("ncol", [128, 4, KC])
    wg_d = [din("wg%d" % i, [D, DFF]) for i in (1, 2)]
    wu_d = [din("wu%d" % i, [D, DFF]) for i in (1, 2)]
    wd_d = [din("wd%d" % i, [DFF, D]) for i in (1, 2)]
    wout_d = din("wout", [D, D])
    out_d = nc.dram_tensor("out", [NT, D], F32, kind="ExternalOutput").ap()
    h1T_d = dint("h1T", [KC, 128, NT])
    if cfg.get("mixer_test"):
        u2T_d = din("u2T_in", [KC, 128, NS], BF16)
        mixT_d = nc.dram_tensor("mixT_out", [KC, 128, NS], BF16, kind="ExternalOutput").ap()
    else:
        u2T_d = dint("u2T", [KC, 128, NS], BF16)
        mixT_d = dint("mixT", [KC, 128, NS], BF16)
    watt_d = [din("watt%d" % g, [D, 640]) for g in range(NG)]
    wdn_d = [din("wdn%d" % g, [D, 1536]) for g in range(NG)]
    wgate_d = [din("wgate%d" % g, [D, 512]) for g in range(NG)]
    wab_d = [din("wab%d" % g, [D, 8]) for g in range(NG)]
    convw_d = [din("convw%d" % g, [128, 12, 4]) for g in range(NG)]
    hrow_d = [din("hrow%d" % g, [1, 16]) for g in range(NG)]
    dnw_d = din("dnw", [1, 128])
    pos_d = din("pos", [128, 32], I32)
    dbg_d = {}
    for nm, shp in dbg:
        dbg_d[nm] = nc.dram_tensor("dbg_" + nm, list(shp), F32, kind="ExternalOutput").ap()

    with ExitStack() as es:
        sems = {e: es.enter_context(nc.semaphore("s_" + e)) for e in ENGS}
        S = Sched(nc, sems, es)

        uniq = [0]

        def sbuf(scope, name, shape, dt):
            uniq[0] += 1
            return scope.enter_context(nc.sbuf_tensor("%s_%d" % (name, uniq[0]), list(shape), dt))

        def psum(scope, name, shape, dt=F32):
            uniq[0] += 1
            return scope.enter_context(nc.psum_tensor("%s_%d" % (name, uniq[0]), list(shape), dt))

        identF = sbuf(es, "identF", [128, 128], F32)
        identB = sbuf(es, "identB", [128, 128], BF16)
        onesB = sbuf(es, "onesB", [128, 128], BF16)
        vec = sbuf(es, "vec", [128, 10, KC], F32)
        b_const = Buf("const")
        b_vec = Buf("vec")
        GW1, SH1, G1H, GW2, SH2, G2, GW3, SH3, G3H, FW = range(10)

        b_h1T = [Buf("h1T%d" % g) for g in range(NGRP)]
        b_u2T = [Buf("u2T%d" % g) for g in range(NGRP)]
        b_mixT = Buf("mixT")

        with ExitStack() as ph, nc.Block() as block:
            idi = sbuf(ph, "idi", [128, 128], I32)
            ccol = sbuf(ph, "ccol_s", [128, KC], F32)
            cact = sbuf(ph, "cact", [128, KC], BF16)
            adab = sbuf(ph, "adab_s", [128, NMOD * KC], F32)
            ncol = sbuf(ph, "ncol_s", [128, 4, KC], F32)
            modc = sbuf(ph, "modc", [128, NMOD * KC], F32)
            JB = 512
            NJB = 0 if cfg.get("mixer_test") else NMOD * D // JB
            awb = [sbuf(ph, "awb%d" % i, [128, KC, JB], BF16) for i in range(2)]
            pmod = psum(ph, "pmod", [128, 512])
            b_idi, b_ccol, b_cact, b_adab, b_ncol, b_modc, b_pmod = (Buf() for _ in range(7))
            b_aw = [[Buf(), Buf()] for _ in range(2)]
            ds_small = [S.new_dsem() for _ in range(4)]
            ds_aw = [[S.new_dsem() for _ in range(2)] for _ in range(2)]

            S.op("pool", lambda e: e.iota(idi[:], pattern=[[1, 128]], base=0, channel_multiplier=-1), writes=[b_idi])
            S.ts("dve", identF[:], idi[:], 0, ALU.is_equal, reads=[b_idi], writes=[b_const])
            S.copy("dve", identB[:], identF[:], reads=[b_const], writes=[b_const])
            S.memset("dve", onesB[:], 1.0, writes=[b_const])
            S.dma("sp", ccol[:], ccol_d, ds_small[0], writes=[b_ccol])
            S.dma("sp", adab[:], adab_d, ds_small[1], writes=[b_adab])
            S.dma("sp", ncol[:], ncol_d, ds_small[2], writes=[b_ncol])
            S.act(cact[:], ccol[:], AF.Silu, reads=[b_ccol], writes=[b_cact])
            awv = adaw_d.rearrange("(kc p) j -> p kc j", p=128)
            for jb in range(NJB):
                s = jb % 2
                for h in range(2):
                    S.dma("pool", awb[s][:, 8 * h:8 * h + 8, :], awv[:, 8 * h:8 * h + 8, jb * JB:(jb + 1) * JB], ds_aw[s][h],
                          writes=[b_aw[s][h]])
                for jc in range(JB // 128):
                    col = jb * (JB // 128) + jc
                    for kc in range(KC):
                        S.mm(pmod[:, col:col + 1], awb[s][:, kc, jc * 128:(jc + 1) * 128], cact[:, kc:kc + 1],
                             start=(kc == 0), stop=(kc == KC - 1), reads=[b_aw[s][kc // 8], b_cact], writes=[b_pmod])
            if NJB:
                S.tt("dve", modc[:], pmod[:, 0:NMOD * KC], adab[:], ALU.add, reads=[b_pmod, b_adab], writes=[b_modc])

                def msl(i):
                    return modc[:, i * KC:(i + 1) * KC]
                for (gw, sh, gg, half, ni, base) in ((GW1, SH1, G1H, 0.5, 0, 0), (GW2, SH2, G2, 1.0, 1, 3), (GW3, SH3, G3H, 0.5, 2, 6)):
                    S.stt("dve", vec[:, gw, :], msl(base + 1), 1.0, ncol[:, ni, :], ALU.add, ALU.mult,
                          reads=[b_modc, b_ncol], writes=[b_vec])
                    S.copy("dve", vec[:, sh, :], msl(base), reads=[b_modc], writes=[b_vec])
                    S.ts("dve", vec[:, gg, :], msl(base + 2), half, ALU.mult, reads=[b_modc], writes=[b_vec])
                S.copy("dve", vec[:, FW, :], ncol[:, 3, :], reads=[b_ncol], writes=[b_vec])
            if "vec" in dbg_d:
                S.dma("sp", dbg_d["vec"], vec[:].rearrange("p a k -> p (a k)"), ds_small[3], reads=[b_vec])
            S.barrier()
            S.emit(block)

        def alloc_ffn(ph):
            T = {}
            T["acc"] = sbuf(ph, "acc", [128, KC, TG], F32)
            T["uT"] = sbuf(ph, "uT", [128, KC, TG], BF16)
            T["wgb"] = [sbuf(ph, "wgb%d" % i, [128, KC, FB], BF16) for i in range(2)]
            T["wub"] = [sbuf(ph, "wub%d" % i, [128, KC, FB], BF16) for i in range(2)]
            T["wdb"] = [sbuf(ph, "wdb%d" % i, [128, FB // 128, D], BF16) for i in range(2)]
            T["actT"] = [sbuf(ph, "actT%d" % i, [128, FB // 128, TG], BF16) for i in range(2)]
            T["sq"] = [sbuf(ph, "sq%d" % i, [128, 512], BF16) for i in range(2)]
            T["rstd"] = sbuf(ph, "rstd", [128, 512], F32)
            T["tmpn"] = [sbuf(ph, "tmpn%d" % i, [128, 512], F32) for i in range(2)]
            T["sil"] = [sbuf(ph, "sil%d" % i, [128, 512], F32) for i in range(2)]
            T["xin"] = [sbuf(ph, "xin%d" % i, [128, D], F32) for i in range(2)]
            T["stg"] = [sbuf(ph, "stg%d" % i, [128, 4, FB], F32) for i in range(4)]
            T["b_stg"] = [Buf() for _ in range(4)]
            T["pg"] = [psum(ph, "pg%d" % i, [128, 512]) for i in range(2)]
            T["pu"] = [psum(ph, "pu%d" % i, [128, 512]) for i in range(2)]
            T["pd"] = [psum(ph, "pd%d" % i, [128, 512]) for i in range(2)]
            T["pm"] = psum(ph, "pm", [128, 512])
            T["pt"] = psum(ph, "pt", [128, 4, 128])
            T["b_acc"] = [[Buf() for _ in range(NTT)] for _ in range(KC)]
            T["b_uT"] = [Buf() for _ in range(NTT)]
            T["b_wg"] = [[Buf(), Buf()] for _ in range(2)]
            T["b_wu"] = [[Buf(), Buf()] for _ in range(2)]
            T["b_wd"] = [[Buf() for _ in range(FB // 128)] for _ in range(2)]
            T["b_actT"] = [[[Buf() for _ in range(NTT)] for _ in range(FB // 128)] for _ in range(2)]
            T["b_sq"] = [Buf(), Buf()]
            T["b_rstd"] = Buf()
            T["b_tmpn"] = [Buf(), Buf()]
            T["b_sil"] = [Buf(), Buf()]
            T["b_xin"] = [Buf(), Buf()]
            T["b_pg"] = [Buf(), Buf()]
            T["b_pu"] = [Buf(), Buf()]
            T["b_pd"] = [Buf(), Buf()]
            T["b_pm"] = Buf()
            T["b_pt"] = Buf()
            T["ds_wg"] = [[S.new_dsem() for _ in range(2)] for _ in range(2)]
            T["ds_wu"] = [[S.new_dsem() for _ in range(2)] for _ in range(2)]
            T["ds_wd"] = [[S.new_dsem() for _ in range(FB // 128)] for _ in range(2)]
            T["ds_io"] = [S.new_dsem() for _ in range(2)]
            T["ds_st"] = [S.new_dsem() for _ in range(2)]
            T["ds_ld"] = [S.new_dsem() for _ in range(2)]
            T["cnt"] = {"gu": 0, "pd": 0, "sq": 0, "tmpn": 0, "xin": 0, "stg": 0}
            return T

        def rms_stats(T, tt):
            acc = T["acc"]
            for kc in range(KC):
                i = T["cnt"]["sq"] % 2
                T["cnt"]["sq"] += 1
                S.act(T["sq"][i][:], acc[:, kc, tt * 512:(tt + 1) * 512], AF.Square,
                      reads=[T["b_acc"][kc][tt]], writes=[T["b_sq"][i]])
                S.mm(T["pm"][:], onesB[:], T["sq"][i][:], start=(kc == 0), stop=(kc == KC - 1),
                     reads=[T["b_sq"][i], b_const], writes=[T["b_pm"]])
            S.act(T["rstd"][:], T["pm"][:], AF.Sqrt, bias=EPS, scale=1.0 / D, reads=[T["b_pm"]], writes=[T["b_rstd"]])
            S.op("dve", lambda e: e.reciprocal(out=T["rstd"][:], in_=T["rstd"][:]), reads=[T["b_rstd"]], writes=[T["b_rstd"]])

        def norm_group(T, gw, sh):
            acc, uT = T["acc"], T["uT"]
            for tt in range(NTT):
                rms_stats(T, tt)
                for kc in range(KC):
                    i = T["cnt"]["tmpn"] % 2
                    T["cnt"]["tmpn"] += 1
                    S.tt("dve", T["tmpn"][i][:], acc[:, kc, tt * 512:(tt + 1) * 512], T["rstd"][:], ALU.mult,
                         reads=[T["b_acc"][kc][tt], T["b_rstd"]], writes=[T["b_tmpn"][i]])
                    S.act(uT[:, kc, tt * 512:(tt + 1) * 512], T["tmpn"][i][:], AF.Identity,
                          bias=vec[:, sh, kc:kc + 1], scale=vec[:, gw, kc:kc + 1],
                          reads=[T["b_tmpn"][i], b_vec], writes=[T["b_uT"][tt]])

        def ffn_group(T, wg, wu, wd, gate):
            acc, uT = T["acc"], T["uT"]
            wgv = wg.rearrange("(kc p) f -> p kc f", p=128)
            wuv = wu.rearrange("(kc p) f -> p kc f", p=128)
            wdv = wd.rearrange("(fc p) o -> p fc o", p=128)
            NFC = FB // 128
            ds_stg = T["ds_wg"][0] + T["ds_wg"][1]

            def gu_pieces(fb):
                s = fb % 2
                out = []
                for wv, wb, bb, eng in ((wgv, T["wgb"], T["b_wg"], "act"), (wuv, T["wub"], T["b_wu"], "dve")):
                    for q in range(4):
                        out.append((wv[:, 4 * q:4 * q + 4, fb * FB:(fb + 1) * FB], wb[s][:, 4 * q:4 * q + 4, :], bb[s][q // 2], eng))
                return out

            def load_piece(p):
                src, dst, buf, eng = p
                j = T["cnt"]["stg"] % 4
                T["cnt"]["stg"] += 1
                S.dma("sp", T["stg"][j][:], src, ds_stg[j], writes=[T["b_stg"][j]])
                S.copy(eng, dst, T["stg"][j][:], reads=[T["b_stg"][j]], writes=[buf])

            for p in gu_pieces(0):
                load_piece(p)
            for fb in range(NFB):
                s = fb % 2
                nxt = gu_pieces(fb + 1) if fb + 1 < NFB else []
                for fc in range(NFC):
                    S.dma("pool", T["wdb"][s][:, fc, :], wdv[:, fb * NFC + fc, :], T["ds_wd"][s][fc], writes=[T["b_wd"][s][fc]])
                for fc in range(NFC):
                    for tt in range(NTT):
                        i = T["cnt"]["gu"] % 2
                        T["cnt"]["gu"] += 1
                        for kc in range(KC):
                            S.mm(T["pg"][i][:], T["wgb"][s][:, kc, fc * 128:(fc + 1) * 128], uT[:, kc, tt * 512:(tt + 1) * 512],
                                 start=(kc == 0), stop=(kc == KC - 1),
                                 reads=[T["b_wg"][s][kc // 8], T["b_uT"][tt]], writes=[T["b_pg"][i]])
                        for kc in range(KC):
                            S.mm(T["pu"][i][:], T["wub"][s][:, kc, fc * 128:(fc + 1) * 128], uT[:, kc, tt * 512:(tt + 1) * 512],
                                 start=(kc == 0), stop=(kc == KC - 1),
                                 reads=[T["b_wu"][s][kc // 8], T["b_uT"][tt]], writes=[T["b_pu"][i]])
                        S.act(T["sil"][i][:], T["pg"][i][:], AF.Silu, reads=[T["b_pg"][i]], writes=[T["b_sil"][i]])
                        S.tt("dve", T["actT"][s][:, fc, tt * 512:(tt + 1) * 512], T["sil"][i][:], T["pu"][i][:], ALU.mult,
                             reads=[T["b_sil"][i], T["b_pu"][i]], writes=[T["b_actT"][s][fc][tt]])
                        for _ in range(2):
                            if nxt:
                                load_piece(nxt.pop(0))
                for oc in range(KC):
                    for tt in range(NTT):
                        i = T["cnt"]["pd"] % 2
                        T["cnt"]["pd"] += 1
                        for fc in range(NFC):
                            S.mm(T["pd"][i][:], T["wdb"][s][:, fc, oc * 128:(oc + 1) * 128],
                                 T["actT"][s][:, fc, tt * 512:(tt + 1) * 512],
                                 start=(fc == 0), stop=(fc == NFC - 1),
                                 reads=[T["b_wd"][s][fc], T["b_actT"][s][fc][tt]], writes=[T["b_pd"][i]])
                        a = acc[:, oc, tt * 512:(tt + 1) * 512]
                        S.stt("dve", a, T["pd"][i][:], vec[:, gate, oc:oc + 1], a, ALU.mult, ALU.add,
                              reads=[T["b_pd"][i], T["b_acc"][oc][tt], b_vec], writes=[T["b_acc"][oc][tt]])

        def store_featmajor(T, dst, g, dt_bufs, src_key, src_bufs_fn, dsem):
            src = T[src_key]
            rd = []
            for kc in range(KC):
                rd += src_bufs_fn(kc)
            pairs = [(dst[4 * q:4 * q + 4, :, g[0]:g[1]].rearrange("k p t -> p k t"), src[:, 4 * q:4 * q + 4, :]) for q in range(4)]
            return S.dmas("sp", pairs, dsem, reads=rd, writes=dt_bufs)

        with ExitStack() as ph, nc.Block() as block:
            T = alloc_ffn(ph)
            for g in range(NGRP):
                for ts_ in range(NSUB):
                    i = T["cnt"]["xin"] % 2
                    T["cnt"]["xin"] += 1
                    r0 = g * TG + ts_ * 128
                    S.dma("sp", T["xin"][i][:], x_d[r0:r0 + 128, :], T["ds_io"][i], writes=[T["b_xin"][i]])
                    for q in range(4):
                        for j in range(4):
                            kc = 4 * q + j
                            S.tr(T["pt"][:, j, :], T["xin"][i][:, kc * 128:(kc + 1) * 128], identF[:],
                                 reads=[T["b_xin"][i], b_const], writes=[T["b_pt"]])
                        S.copy("act" if q % 2 else "dve", T["acc"][:, 4 * q:4 * q + 4, ts_ * 128:(ts_ + 1) * 128], T["pt"][:],
                               reads=[T["b_pt"]], writes=[T["b_acc"][kc][ts_ // 4] for kc in range(4 * q, 4 * q + 4)])
                norm_group(T, GW1, SH1)
                ffn_group(T, wg_d[0], wu_d[0], wd_d[0], G1H)
                store_featmajor(T, h1T_d, (g * TG, (g + 1) * TG), [b_h1T[g]], "acc",
                                lambda kc: [T["b_acc"][kc][tt] for tt in range(NTT)], T["ds_st"][0])
                norm_group(T, GW2, SH2)
                store_featmajor(T, u2T_d, (TOK0 + g * TG, TOK0 + (g + 1) * TG), [b_u2T[g]], "uT",
                                lambda kc: [T["b_uT"][tt] for tt in range(NTT)], T["ds_st"][1])
            S.barrier()
            S.emit(block)

        def u2_view(st):
            return u2T_d[:, :, st * 512:(st + 1) * 512].rearrange("k p t -> p k t")

        def build_rope_tables(ph, cosT, sinT, b_tab):
            posi = sbuf(ph, "posi", [128, 32], I32)
            posf = sbuf(ph, "posf", [128, 32], F32)
            invf = sbuf(ph, "invf", [128, 8], F32)
            ang = sbuf(ph, "ang", [128, 32, 8], F32)
            tf = sbuf(ph, "tf", [128, 32, 8], F32)
            ti = sbuf(ph, "ti", [128, 32, 8], I32)
            b_p, b_if, b_ang, b_tf, b_ti = Buf(), Buf(), Buf(), Buf(), Buf()
            ds = S.new_dsem()
            S.dma("sp", posi[:], pos_d, ds, writes=[b_p])
            S.copy("dve", posf[:], posi[:], reads=[b_p], writes=[b_p])
            for j in range(8):
                S.memset("dve", invf[:, j:j + 1], float(np.float32(500000.0 ** (-j / 8.0))), writes=[b_if])
            for which, dst in ((0, sinT), (1, cosT)):
                S.tt("dve", ang[:], posf[:].unsqueeze(2).to_broadcast([128, 32, 8]),
                     invf[:].unsqueeze(1).to_broadcast([128, 32, 8]), ALU.mult, reads=[b_p, b_if], writes=[b_ang])
                if which == 1:
                    S.ts("dve", ang[:], ang[:], float(np.pi / 2), ALU.add, reads=[b_ang], writes=[b_ang])
                S.ts("dve", tf[:], ang[:], 1.0 / TWO_PI, ALU.mult, reads=[b_ang], writes=[b_tf])
                S.copy("dve", ti[:], tf[:], reads=[b_tf], writes=[b_ti])
                S.copy("dve", tf[:], ti[:], reads=[b_ti], writes=[b_tf])
                S.stt("dve", ang[:], tf[:], -TWO_PI, ang[:], ALU.mult, ALU.add, reads=[b_tf, b_ang], writes=[b_ang])
                S.ts("dve", tf[:], ang[:], float(np.pi), ALU.is_gt, s2=-TWO_PI, op1=ALU.mult, reads=[b_ang], writes=[b_tf])
                S.tt("dve", ang[:], ang[:], tf[:], ALU.add, reads=[b_ang, b_tf], writes=[b_ang])
                S.ts("dve", ang[:], ang[:], float(np.pi), ALU.min, s2=-float(np.pi), op1=ALU.max, reads=[b_ang], writes=[b_ang])
                S.act(dst[:], ang[:], AF.Sin, reads=[b_ang], writes=[b_tab])

        def attn_pass(gi):
            with ExitStack() as ph, nc.Block() as block:
                watt = sbuf(ph, "watt", [128, KC, 640], BF16)
                u2s = [sbuf(ph, "u2s%d" % i, [128, KC, 512], BF16) for i in range(2)]
                kT2 = sbuf(ph, "kT2", [128, 33 * 128], BF16)
                Vall = sbuf(ph, "Vall", [128, 33, 64], BF16)
                cosT = sbuf(ph, "cosT", [128, 32, 8], F32)
                sinT = sbuf(ph, "sinT", [128, 32, 8], F32)
                hb = sbuf(ph, "hb", [128, 16], F32)
                idi = sbuf(ph, "idi256", [128, 256], I32)
                mv1 = sbuf(ph, "mv1", [128, 256], F32)
                mv2 = sbuf(ph, "mv2", [128, 256], F32)
                maskN = sbuf(ph, "maskN", [128, 256], F32)
                mask0 = sbuf(ph, "mask0", [128, 256], F32)
                qr = sbuf(ph, "qr", [128, 512], BF16)
                kr2 = sbuf(ph, "kr2", [128, 128], BF16)
                rt = [sbuf(ph, "rt%d" % i, [128, 8, 8], F32) for i in range(4)]
                rk = [sbuf(ph, "rk%d" % i, [128, 1, 8], F32) for i in range(4)]
                qT = sbuf(ph, "qT", [128, 4, 128], BF16)
                qTz = [sbuf(ph, "qTz%d" % i, [128, 4, 128], BF16) for i in range(2)]
                sm = [sbuf(ph, "sm%d" % i, [128, 2, 256], F32) for i in range(2)]
                pexp = [sbuf(ph, "pexp%d" % i, [128, 256], BF16) for i in range(4)]
                PT = [sbuf(ph, "PT%d" % i, [128, 2, 128], BF16) for i in range(4)]
                mx = sbuf(ph, "mx", [128, 8], F32)
                mneg = sbuf(ph, "mneg", [128, 8], F32)
                rs = sbuf(ph, "rs", [128, 8], F32)
                es_ = sbuf(ph, "es", [128, 8], F32)
                rinv = sbuf(ph, "rinv", [128, 8], F32)
                ao = sbuf(ph, "ao", [128, 512], BF16)
                aoT = [sbuf(ph, "aoT%d" % i, [128, 4, 512], BF16) for i in range(2)]
                pq = psum(ph, "pq", [128, 512])
                pkv = psum(ph, "pkv", [128, 512])
                ptr = psum(ph, "ptr", [128, 8, 128], BF16)
                ps = [psum(ph, "ps%d" % i, [128, 2, 256]) for i in range(2)]
                ppt = [psum(ph, "ppt%d" % i, [128, 8, 128], BF16) for i in range(2)]
                po = psum(ph, "po", [128, 8, 64])
                b_watt = [Buf(), Buf()]
                b_u2s = [[Buf(), Buf()] for _ in range(2)]
                b_kT2 = [Buf() for _ in range(33)]
                b_V = [Buf() for _ in range(33)]
                b_tab, b_hb, b_mask, b_idi, b_mv = Buf(), Buf(), Buf(), Buf(), Buf()
                b_qr, b_kr2, b_qT, b_ao = Buf(), Buf(), Buf(), Buf()
                b_qr_a, b_kr2_a, b_kr2_c = Buf(), Buf(), Buf()
                b_rt = [Buf() for _ in range(4)]
                b_rk = [Buf() for _ in range(4)]
                b_sm = [Buf(), Buf()]
                b_pexp = [Buf() for _ in range(4)]
                b_PT = [Buf() for _ in range(4)]
                b_mx, b_mneg, b_rs, b_es, b_rinv = [[Buf() for _ in range(4)] for _ in range(5)]
                b_aoT = [Buf(), Buf()]
                b_pq, b_pkv, b_ptr = Buf(excl=True), Buf(excl=True), Buf(excl=True)
                _bpo = Buf(excl=True)
                b_po = [_bpo] * 4
                b_ps = [Buf(excl=True), Buf(excl=True)]
                b_ppt = [Buf(excl=True), Buf(excl=True)]
                ds_w = [S.new_dsem() for _ in range(2)]
                ds_u = [[S.new_dsem() for _ in range(2)] for _ in range(2)]
                ds_m = S.new_dsem()
                ds_o = [S.new_dsem() for _ in range(2)]

                wv = watt_d[gi].rearrange("(kc p) j -> p kc j", p=128)
                import os as _os
                _skip = _os.environ.get("ATT_SKIP", "")
                if "w" not in _skip:
                  for h in range(2):
                    S.dma("pool", watt[:, 8 * h:8 * h + 8, :], wv[:, 8 * h:8 * h + 8, :], ds_w[h], writes=[b_watt[h]])
                if "h" not in _skip:
                    S.dma("sp", hb[:], hrow_d[gi].partition_broadcast(128), ds_m, writes=[b_hb])
                if "r" not in _skip:
                    build_rope_tables(ph, cosT, sinT, b_tab)
                if "m" in _skip:
                    S.barrier()
                    S.emit(block)
                    return
                S.op("pool", lambda e: e.iota(idi[:], pattern=[[1, 256]], base=0, channel_multiplier=-1), writes=[b_idi])
                S.ts("dve", mv1[:], idi[:], 1, ALU.is_ge, reads=[b_idi], writes=[b_mv])
                S.ts("dve", mv2[:], idi[:], 129, ALU.is_ge, reads=[b_idi], writes=[b_mv])
                S.tt("dve", mv1[:], mv1[:], mv2[:], ALU.subtract, reads=[b_mv], writes=[b_mv])
                S.ts("dve", maskN[:], mv1[:], -1.0, ALU.add, s2=30000.0, op1=ALU.mult, reads=[b_mv], writes=[b_mask])
                S.copy("dve", mask0[:], maskN[:], reads=[b_mask], writes=[b_mask])
                S.memset("dve", mask0[:, 0:128], -30000.0, writes=[b_mask])
                S.memset("dve", qTz[0][:], 0.0, writes=[b_qT])
                S.memset("dve", qTz[1][:], 0.0, writes=[b_qT])
                S.memset("dve", kT2[:, 0:128], 0.0, writes=[b_kT2[0]])
                S.memset("dve", Vall[:, 0, :], 0.0, writes=[b_V[0]])

                def rope(src3, dst3, nh, tmp, b_tmp, cos_i, sin_i, b_src, b_dst):
                    cb = cos_i.unsqueeze(1).to_broadcast([128, nh, 8])
                    sb_ = sin_i.unsqueeze(1).to_broadcast([128, nh, 8])
                    S.tt("dve", tmp[0][:], src3[:, :, 0:8], cb, ALU.mult, reads=[b_src, b_tab], writes=[b_tmp[0]])
                    S.tt("dve", tmp[1][:], src3[:, :, 8:16], sb_, ALU.mult, reads=[b_src, b_tab], writes=[b_tmp[1]])
                    S.tt("dve", tmp[2][:], src3[:, :, 8:16], cb, ALU.mult, reads=[b_src, b_tab], writes=[b_tmp[2]])
                    S.tt("dve", tmp[3][:], src3[:, :, 0:8], sb_, ALU.mult, reads=[b_src, b_tab], writes=[b_tmp[3]])
                    S.tt("dve", dst3[:, :, 0:8], tmp[0][:], tmp[1][:], ALU.subtract, reads=[b_tmp[0], b_tmp[1]], writes=[b_dst])
                    S.tt("dve", dst3[:, :, 8:16], tmp[2][:], tmp[3][:], ALU.add, reads=[b_tmp[2], b_tmp[3]], writes=[b_dst])

                _stop = int(_os.environ.get("ATT_STOP", "99"))
                for i in range(cfg.get("attn_tiles", 32)):
                    st, sub = i // 4, i % 4
                    sb2 = st % 2
                    if sub == 0:
                        for h in range(2):
                            S.dma("sp", u2s[sb2][:, 8 * h:8 * h + 8, :], u2_view(st)[:, 8 * h:8 * h + 8, :], ds_u[sb2][h],
                                  reads=[b_u2T[g] for g in range(NGRP)] if not cfg.get("mixer_test") else [],
                                  writes=[b_u2s[sb2][h]])
                    for kc in range(KC):
                        S.mm(pq[:], u2s[sb2][:, kc, sub * 128:(sub + 1) * 128], watt[:, kc, 0:512], start=(kc == 0), stop=(kc == KC - 1),
                             reads=[b_u2s[sb2][kc // 8], b_watt[kc // 8]], writes=[b_pq])
                    for kc in range(KC):
                        S.mm(pkv[:, 0:128], u2s[sb2][:, kc, sub * 128:(sub + 1) * 128], watt[:, kc, 512:640], start=(kc == 0), stop=(kc == KC - 1),
                             reads=[b_u2s[sb2][kc // 8], b_watt[kc // 8]], writes=[b_pkv])
                    if _stop == 1:
                        break
                    pq3 = pq[:].rearrange("p (h d) -> p h d", h=8)
                    qr3 = qr[:].rearrange("p (h d) -> p h d", h=8)
                    _sub = _os.environ.get("ATT_SUB", "adkv")
                    if "a" in _sub:
                        S.copy("dve", qr[:], pq[:], reads=[b_pq], writes=[b_qr])
                    if "d" in _sub:
                        rope(pq3, qr3, 8, rt, b_rt, cosT[:, i, :], sinT[:, i, :], b_pq, b_qr)
                    pk3 = pkv[:, 0:64].rearrange("p (h d) -> p h d", h=1)
                    kr3 = kr2[:, 0:64].rearrange("p (h d) -> p h d", h=1)
                    if "k" in _sub:
                        S.copy("dve", kr2[:, 0:64], pkv[:, 0:64], reads=[b_pkv], writes=[b_kr2])
                        rope(pk3, kr3, 1, rk, b_rk, cosT[:, i, :], sinT[:, i, :], b_pkv, b_kr2)
                        S.copy("dve", kr2[:, 64:128], kr2[:, 0:64], reads=[b_kr2, b_kr2_a], writes=[b_kr2_c])
                    if "v" in _sub:
                        S.copy("dve", Vall[:, i + 1, :], pkv[:, 64:128], reads=[b_pkv], writes=[b_V[i + 1]])
                    if _stop == 2:
                        break
                    for hp in range(4):
                        S.tr(ptr[:, hp, :], qr[:, hp * 128:(hp + 1) * 128], identB[:], reads=[b_qr, b_qr_a, b_const], writes=[b_ptr])
                    S.tr(ptr[:, 4, :], kr2[:], identB[:], reads=[b_kr2, b_kr2_a, b_kr2_c, b_const], writes=[b_ptr])
                    S.copy("dve", qTz[0][0:64, :, :], ptr[0:64, 0:4, :], reads=[b_ptr], writes=[b_qT])
                    S.copy("dve", qTz[1][64:128, :, :], ptr[64:128, 0:4, :], reads=[b_ptr], writes=[b_qT])
                    S.copy("dve", kT2[:, (i + 1) * 128:(i + 2) * 128], ptr[:, 4, :], reads=[b_ptr], writes=[b_kT2[i + 1]])
                    if _stop == 3:
                        break
                    mask = mask0 if i == 0 else maskN
                    for hp in range(4):
                        b2 = hp % 2
                        for e_ in range(2):
                            base = 64 * e_
                            S.mm(ps[b2][:, e_, :], qTz[e_][:, hp, :], kT2[:, i * 128:(i + 2) * 128],
                                 reads=[b_qT, b_kT2[i], b_kT2[i + 1]], writes=[b_ps[b2]])
                        S.tt("dve", sm[b2][:], ps[b2][:], mask[:].unsqueeze(1).to_broadcast([128, 2, 256]), ALU.add,
                             reads=[b_ps[b2], b_mask], writes=[b_sm[b2]])
                        if _stop == 40:
                            break
                        S.op("dve", (lambda b2=b2, hp=hp: (lambda e: e.tensor_reduce(out=mx[:, 2 * hp:2 * hp + 2], in_=sm[b2][:], axis=AX.X, op=ALU.max)))(),
                             reads=[b_sm[b2]], writes=[b_mx[hp]])
                        if _stop == 41:
                            break
                        S.stt("dve", mneg[:, 2 * hp:2 * hp + 2], mx[:, 2 * hp:2 * hp + 2], 0.125, hb[:, 8 + 2 * hp:10 + 2 * hp], ALU.mult, ALU.max,
                              reads=[b_mx[hp], b_hb], writes=[b_mneg[hp]])
                        S.ts("dve", mneg[:, 2 * hp:2 * hp + 2], mneg[:, 2 * hp:2 * hp + 2], -1.0, ALU.mult, reads=[b_mneg[hp]], writes=[b_mneg[hp]])
                        S.tt("dve", es_[:, 2 * hp:2 * hp + 2], hb[:, 8 + 2 * hp:10 + 2 * hp], mneg[:, 2 * hp:2 * hp + 2], ALU.add,
                             reads=[b_hb, b_mneg[hp]], writes=[b_es[hp]])
                        if _stop == 4:
                            break
                        S.memset("dve", rs[:, 2 * hp:2 * hp + 2], 0.0, writes=[b_rs[hp]])
                        for e_ in range(2):
                            h = 2 * hp + e_
                            pi = (2 * hp + e_) % 4
                            S.act(pexp[pi][:], sm[b2][:, e_, :], AF.Exp, bias=mneg[:, h:h + 1], scale=0.125, accum=rs[:, h:h + 1],
                                  reads=[b_sm[b2], b_mneg[hp]], writes=[b_pexp[pi], b_rs[hp]])
                        S.act(es_[:, 2 * hp:2 * hp + 2], es_[:, 2 * hp:2 * hp + 2], AF.Exp, reads=[b_es[hp]], writes=[b_es[hp]])
                        S.tt("dve", rinv[:, 2 * hp:2 * hp + 2], rs[:, 2 * hp:2 * hp + 2], es_[:, 2 * hp:2 * hp + 2], ALU.add,
                             reads=[b_rs[hp], b_es[hp]], writes=[b_rinv[hp]])
                        S.op("dve", (lambda hp=hp: (lambda e: e.reciprocal(out=rinv[:, 2 * hp:2 * hp + 2], in_=rinv[:, 2 * hp:2 * hp + 2])))(),
                             reads=[b_rinv[hp]], writes=[b_rinv[hp]])
                        if _stop == 5:
                            break
                        for e_ in range(2):
                            h = 2 * hp + e_
                            pi = (2 * hp + e_) % 4
                            for kt in range(2):
                                S.tr(ppt[e_][:, kt, :], pexp[pi][:, kt * 128:(kt + 1) * 128], identB[:],
                                     reads=[b_pexp[pi], b_const], writes=[b_ppt[e_]])
                            S.copy("act" if e_ else "dve", PT[pi][:], ppt[e_][:, 0:2, :], reads=[b_ppt[e_]], writes=[b_PT[pi]])
                            for kt in range(2):
                                S.mm(po[:, h, :], PT[pi][:, kt, :], Vall[:, i + kt, :], start=(kt == 0), stop=(kt == 1),
                                     reads=[b_PT[pi], b_V[i + kt]], writes=[b_po[hp]])
                            S.act(ao[:, h * 64:(h + 1) * 64], po[:, h, :], AF.Identity, scale=rinv[:, h:h + 1],
                                  reads=[b_po[hp], b_rinv[hp]], writes=[b_ao])
                    if _stop in (4, 40, 41, 5, 6):
                        break
                    for hp in range(4):
                        S.tr(ptr[:, hp, :], ao[:, hp * 128:(hp + 1) * 128], identB[:], reads=[b_ao, b_const], writes=[b_ptr])
                    S.copy("dve", aoT[sb2][:, :, sub * 128:(sub + 1) * 128], ptr[:, 0:4, :], reads=[b_ptr], writes=[b_aoT[sb2]])
                    if sub == 3:
                        S.dma("sp", mixT_d[gi * 8:gi * 8 + 4, :, st * 512:(st + 1) * 512].rearrange("k p t -> p k t"), aoT[sb2][:],
                              ds_o[sb2], reads=[b_aoT[sb2]], writes=[b_mixT_parts[gi][0]])
                S.barrier()
                S.emit(block)

        def dn_pass(gi):
            with ExitStack() as ph, nc.Block() as block:
                H = 4
                wdn = sbuf(ph, "wdn", [128, KC, 1536], BF16)
                wgt = sbuf(ph, "wgt", [128, KC, 512], BF16)
                wab = sbuf(ph, "wab", [128, KC, 8], BF16)
                u2s = [sbuf(ph, "u2d%d" % i, [128, KC, 512], BF16) for i in range(2)]
                cw = sbuf(ph, "cw", [128, 12, 4], F32)
                hb = sbuf(ph, "hbd", [128, 16], F32)
                nwb = sbuf(ph, "nwb", [128, 128], F32)
                nea = sbuf(ph, "nea", [128, 4], F32)
                zb = [sbuf(ph, "zb%d" % i, [128, 515], F32) for i in range(2)]
                halo = sbuf(ph, "halo", [128, 12, 3], F32)
                cacc = [sbuf(ph, "cacc%d" % i, [128, 512], F32) for i in range(2)]
                ysil = [sbuf(ph, "ysil%d" % i, [128, 512], F32) for i in range(2)]
                sqb = [sbuf(ph, "sqb%d" % i, [128, 512], BF16) for i in range(2)]
                rnb = [sbuf(ph, "rnb%d" % i, [128, 512], F32) for i in range(2)]
                QT = sbuf(ph, "QT", [128, H, 512], BF16)
                KT = sbuf(ph, "KT", [128, H, 512], BF16)
                VT = sbuf(ph, "VT", [128, H, 512], F32)
                idi = sbuf(ph, "idid", [128, 128], I32)
                Tri2 = sbuf(ph, "Tri2", [128, 128], F32)
                MMs = sbuf(ph, "MMs", [128, 128], F32)
                onesF = sbuf(ph, "onesF", [128, 128], F32)
                ab = sbuf(ph, "ab", [128, 4, 8], F32)
                xp = sbuf(ph, "xp", [128, 4, 4], F32)
                xn = sbuf(ph, "xn", [128, 4, 4], F32)
                gstep = sbuf(ph, "gstep", [128, 4, 4], F32)
                beta = sbuf(ph, "beta", [128, 4, 4], F32)
                nbeta = sbuf(ph, "nbeta", [128, 4, 4], F32)
                gcum = sbuf(ph, "gcum", [128, 4, 4], F32)
                egc = sbuf(ph, "egc", [128, 4, 4], F32)
                bge = sbuf(ph, "bge", [128, 4, 4], F32)
                gld = sbuf(ph, "gld", [128, 4, 4], F32)
                gws = [sbuf(ph, "gws%d" % i, [128, 512], F32) for i in range(2)]
                gb = [sbuf(ph, "gb%d" % i, [128, 128], F32) for i in range(2)]
                EG = [sbuf(ph, "EG%d" % i, [128, 128], F32) for i in range(2)]
                dls = [sbuf(ph, "dls%d" % i, [128, 128], F32) for i in range(2)]
                dl = [sbuf(ph, "dl%d" % i, [128, 128], F32) for i in range(2)]
                Xs = [sbuf(ph, "Xs%d" % i, [128, 128], F32) for i in range(2)]
                Ys = [sbuf(ph, "Ys%d" % i, [128, 128], F32) for i in range(2)]
                TTs = [sbuf(ph, "TTs%d" % i, [128, 128], F32) for i in range(2)]
                TTb = sbuf(ph, "TTb", [128, 128], BF16)
                a_sb = sbuf(ph, "a_sb", [128, 128], BF16)
                aT = sbuf(ph, "aT", [128, 128], BF16)
                Vb = sbuf(ph, "Vb", [128, 128], BF16)
                Kbg = sbuf(ph, "Kbg", [128, 128], BF16)
                kdec = sbuf(ph, "kdec", [128, 128], BF16)
                u_sb = sbuf(ph, "u_sb", [128, 128], F32)
                wT = sbuf(ph, "wT", [128, 128], BF16)
                QgT = sbuf(ph, "QgT", [128, 128], BF16)
                vnw = [sbuf(ph, "vnw%d" % i, [128, 128], BF16) for i in range(2)]
                S32 = sbuf(ph, "S32", [128, H, 128], F32)
                Sb = sbuf(ph, "Sb", [128, H, 128], BF16)
                osq = sbuf(ph, "osq", [128, 128], F32)
                oss = sbuf(ph, "oss", [128, 1], F32)
                on = [sbuf(ph, "on%d" % i, [128, 512], BF16) for i in range(2)]
                onT = [sbuf(ph, "onT%d" % i, [128, H, 512], BF16) for i in range(2)]
                pz = [psum(ph, "pz%d" % i, [128, 512]) for i in range(2)]
                pss = psum(ph, "pss", [128, 512])
                pab = psum(ph, "pab", [128, 512])
                pGf = psum(ph, "pG", [128, 512])
                pG = pGf[:, 0:128]
                pn = psum(ph, "pn", [128, 4, 128])
                pnb = psum(ph, "pnb", [128, 8, 128], BF16)
                psc = psum(ph, "psc", [128, 4, 128])
                B = {}

                def bf(name):
                    if name not in B:
                        B[name] = Buf(name, excl=name in ("pz0", "pz1", "pss", "pab", "pG", "pn", "pnb", "psc"))
                    return B[name]
                ds_w = [S.new_dsem() for _ in range(7)]
                ds_u = [[S.new_dsem() for _ in range(2)] for _ in range(2)]
                ds_m = [S.new_dsem() for _ in range(3)]
                ds_o = [S.new_dsem() for _ in range(2)]

                wv = wdn_d[gi].rearrange("(kc p) j -> p kc j", p=128)
                for q in range(4):
                    S.dma("pool", wdn[:, 4 * q:4 * q + 4, :], wv[:, 4 * q:4 * q + 4, :], ds_w[q], writes=[bf("wdn%d" % q)])
                gv = wgate_d[gi].rearrange("(kc p) j -> p kc j", p=128)
                for h in range(2):
                    S.dma("pool", wgt[:, 8 * h:8 * h + 8, :], gv[:, 8 * h:8 * h + 8, :], ds_w[4 + h], writes=[bf("wgt%d" % h)])
                S.dma("pool", wab[:], wab_d[gi].rearrange("(kc p) j -> p kc j", p=128), ds_w[6], writes=[bf("wab")])
                S.dma("sp", cw[:], convw_d[gi], ds_m[0], writes=[bf("cw")])
                S.dma("sp", hb[:], hrow_d[gi].partition_broadcast(128), ds_m[1], writes=[bf("hb")])
                S.dma("sp", nwb[:], dnw_d.partition_broadcast(128), ds_m[2], writes=[bf("nwb")])
                S.act(nea[:], hb[:, 0:4], AF.Exp, reads=[bf("hb")], writes=[bf("nea")])
                S.ts("dve", nea[:], nea[:], -1.0, ALU.mult, reads=[bf("nea")], writes=[bf("nea")])
                S.op("pool", lambda e: e.iota(idi[:], pattern=[[1, 128]], base=0, channel_multiplier=-1), writes=[bf("idi")])
                S.ts("dve", Tri2[:], idi[:], 0, ALU.is_ge, reads=[bf("idi")], writes=[bf("masks")])
                S.memset("dve", Tri2[0:64, 64:128], 0.0, writes=[bf("masks")])
                S.ts("dve", MMs[:], idi[:], 0, ALU.is_ge, s2=-200.0, op1=ALU.mult, reads=[bf("idi")], writes=[bf("masks")])
                S.memset("dve", MMs[64:128, 0:64], -200.0, writes=[bf("masks")])
                S.memset("dve", onesF[:], 1.0, writes=[bf("masks")])
                S.memset("dve", halo[:], 0.0, writes=[bf("halo")])
                S.memset("dve", vnw[0][:], 0.0, writes=[bf("vnew0")])
                S.memset("dve", vnw[1][:], 0.0, writes=[bf("vnew1")])
                S.memset("dve", S32[:], 0.0, writes=[bf("S32_%d" % h) for h in range(H)])
                S.memset("dve", Sb[:], 0.0, writes=[bf("Sb_%d" % h) for h in range(H)])
                cnt = {"z": 0, "n": 0, "nb": 0, "sc": 0, "pp": 0}

                def pn_slot():
                    cnt["n"] += 1
                    j = cnt["n"] % 4
                    return pn[:, j, :], bf("pn")

                def pnb_slot():
                    cnt["nb"] += 1
                    j = cnt["nb"] % 4
                    return pnb[:, j, :], bf("pnb")

                def psc_slot():
                    cnt["sc"] += 1
                    j = cnt["sc"] % 4
                    return psc[:, j, :], bf("psc")

                for st in range(cfg.get("dn_st", 8)):
                    sb2 = st % 2
                    for h in range(2):
                        S.dma("sp", u2s[sb2][:, 8 * h:8 * h + 8, :], u2_view(st)[:, 8 * h:8 * h + 8, :], ds_u[sb2][h],
                              writes=[bf("u2s%d_%d" % (sb2, h))])
                    u2r = [bf("u2s%d_0" % sb2), bf("u2s%d_1" % sb2)]
                    for c in range(12):
                        typ, hh = c // 4, c % 4
                        zi = cnt["z"] % 2
                        cnt["z"] += 1
                        for kc in range(KC):
                            S.mm(pz[zi][:], wdn[:, kc, c * 128:(c + 1) * 128], u2s[sb2][:, kc, :], start=(kc == 0), stop=(kc == KC - 1),
                                 reads=[bf("wdn%d" % (kc // 4)), u2r[kc // 8]], writes=[bf("pz%d" % zi)])
                        S.copy("act", zb[zi][:, 3:515], pz[zi][:], reads=[bf("pz%d" % zi)], writes=[bf("zb%d" % zi)])
                        S.copy("dve", zb[zi][:, 0:3], halo[:, c, :], reads=[bf("halo")], writes=[bf("zbh%d" % zi)])
                        zr = [bf("zb%d" % zi), bf("zbh%d" % zi)]
                        S.ts("dve", cacc[zi][:], zb[zi][:, 3:515], cw[:, c, 3:4], ALU.mult, reads=zr + [bf("cw")], writes=[bf("cacc%d" % zi)])
                        for j in range(3):
                            S.stt("dve", cacc[zi][:], zb[zi][:, j:j + 512], cw[:, c, j:j + 1], cacc[zi][:], ALU.mult, ALU.add,
                                  reads=zr + [bf("cw"), bf("cacc%d" % zi)], writes=[bf("cacc%d" % zi)])
                        S.copy("dve", halo[:, c, :], zb[zi][:, 512:515], reads=[bf("zb%d" % zi)], writes=[bf("halo")])
                        if typ == 2:
                            S.act(VT[:, hh, :], cacc[zi][:], AF.Silu, reads=[bf("cacc%d" % zi)], writes=[bf("VT%d" % hh)])
                        else:
                            S.act(ysil[zi][:], cacc[zi][:], AF.Silu, reads=[bf("cacc%d" % zi)], writes=[bf("ysil%d" % zi)])
                            S.act(sqb[zi][:], ysil[zi][:], AF.Square, reads=[bf("ysil%d" % zi)], writes=[bf("sqb%d" % zi)])
                            S.mm(pss[:], onesB[:], sqb[zi][:], reads=[bf("sqb%d" % zi), b_const], writes=[bf("pss")])
                            S.act(rnb[zi][:], pss[:], AF.Sqrt, bias=EPS, scale=1.0, reads=[bf("pss")], writes=[bf("rnb%d" % zi)])
                            S.op("dve", (lambda zi=zi: (lambda e: e.reciprocal(out=rnb[zi][:], in_=rnb[zi][:])))(),
                                 reads=[bf("rnb%d" % zi)], writes=[bf("rnb%d" % zi)])
                            if typ == 0:
                                S.stt("dve", QT[:, hh, :], ysil[zi][:], float(128 ** -0.5), rnb[zi][:], ALU.mult, ALU.mult,
                                      reads=[bf("ysil%d" % zi), bf("rnb%d" % zi)], writes=[bf("QT%d" % hh)])
                            else:
                                S.tt("dve", KT[:, hh, :], ysil[zi][:], rnb[zi][:], ALU.mult,
                                     reads=[bf("ysil%d" % zi), bf("rnb%d" % zi)], writes=[bf("KT%d" % hh)])
                    for sub in range(4):
                        for kc in range(KC):
                            S.mm(pab[:, sub * 8:(sub + 1) * 8], u2s[sb2][:, kc, sub * 128:(sub + 1) * 128], wab[:, kc, :],
                                 start=(kc == 0), stop=(kc == KC - 1), reads=[u2r[kc // 8], bf("wab")], writes=[bf("pab")])
                    S.copy("dve", ab[:], pab[:, 0:32].rearrange("p (s j) -> p s j", s=4), reads=[bf("pab")], writes=[bf("ab")])
                    dtb = hb[:, 4:8].unsqueeze(1).to_broadcast([128, 4, 4])
                    S.tt("dve", xp[:], ab[:, :, 0:4], dtb, ALU.add, reads=[bf("ab"), bf("hb")], writes=[bf("xp")])
                    S.ts("dve", xn[:], xp[:], 0.0, ALU.min, reads=[bf("xp")], writes=[bf("xn")])
                    S.ts("dve", xp[:], xp[:], 0.0, ALU.max, reads=[bf("xp")], writes=[bf("xp")])
                    S.tt("dve", xn[:], xn[:], xp[:], ALU.subtract, reads=[bf("xn"), bf("xp")], writes=[bf("xn")])
                    S.act(xn[:], xn[:], AF.Exp, reads=[bf("xn")], writes=[bf("xn")])
                    S.act(xn[:], xn[:], AF.Ln, bias=1.0, reads=[bf("xn")], writes=[bf("xn")])
                    S.tt("dve", xp[:], xp[:], xn[:], ALU.add, reads=[bf("xn"), bf("xp")], writes=[bf("xp")])
                    S.tt("dve", gstep[:], xp[:], nea[:].unsqueeze(1).to_broadcast([128, 4, 4]), ALU.mult,
                         reads=[bf("xp"), bf("nea")], writes=[bf("gstep")])
                    S.act(beta[:], ab[:, :, 4:8], AF.Exp, scale=-1.0, reads=[bf("ab")], writes=[bf("beta")])
                    S.ts("dve", beta[:], beta[:], 1.0, ALU.add, reads=[bf("beta")], writes=[bf("beta")])
                    S.op("dve", lambda e: e.reciprocal(out=beta[:], in_=beta[:]), reads=[bf("beta")], writes=[bf("beta")])
                    S.ts("dve", nbeta[:], beta[:], -1.0, ALU.mult, reads=[bf("beta")], writes=[bf("nbeta")])
                    S.mm(pab[:, 32:48], Tri2[:], gstep[:].rearrange("p s h -> p (s h)"), reads=[bf("masks"), bf("gstep")], writes=[bf("pab")])
                    S.copy("dve", gcum[:], pab[:, 32:48].rearrange("p (s h) -> p s h", s=4), reads=[bf("pab")], writes=[bf("gcum")])
                    S.act(egc[:], gcum[:], AF.Exp, reads=[bf("gcum")], writes=[bf("egc")])
                    S.tt("dve", bge[:], egc[:], beta[:], ALU.mult, reads=[bf("egc"), bf("beta")], writes=[bf("bge")])
                    for sub in range(4):
                        gi2 = sub % 2
                        for kc in range(KC):
                            S.mm(pss[:], u2s[sb2][:, kc, sub * 128:(sub + 1) * 128], wgt[:, kc, :], start=(kc == 0), stop=(kc == KC - 1),
                                 reads=[u2r[kc // 8], bf("wgt%d" % (kc // 8))], writes=[bf("pss")])
                        S.act(gws[gi2][:], pss[:], AF.Silu, reads=[bf("pss")], writes=[bf("gws%d" % gi2)])
                        S.tt("dve", gws[gi2][:].rearrange("p (h v) -> p h v", h=H), gws[gi2][:].rearrange("p (h v) -> p h v", h=H),
                             nwb[:].unsqueeze(1).to_broadcast([128, H, 128]), ALU.mult,
                             reads=[bf("gws%d" % gi2), bf("nwb")], writes=[bf("gws%d" % gi2)])
                        tsl = slice(sub * 128, (sub + 1) * 128)
                        for hh in range(H):
                            x2 = (sub * H + hh) % 2
                            gcol = gcum[:, sub, hh:hh + 1]
                            S.ts("dve", gb[x2][:], onesF[:], gstep[:, sub, hh:hh + 1], ALU.mult, reads=[bf("masks"), bf("gstep")], writes=[bf("gb%d" % x2)])
                            S.mm(pG, gb[x2][:], Tri2[:], reads=[bf("gb%d" % x2), bf("masks")], writes=[bf("pG")])
                            S.act(EG[x2][:], pG, AF.Exp, reads=[bf("pG")], writes=[bf("EG%d" % x2)])
                            S.ts("dve", dls[x2][:], pG, -1.0, ALU.mult, s2=gcol, op1=ALU.add, reads=[bf("pG"), bf("gcum")], writes=[bf("dls%d" % x2)])
                            S.tt("dve", dls[x2][:], dls[x2][:], MMs[:], ALU.min, reads=[bf("dls%d" % x2), bf("masks")], writes=[bf("dls%d" % x2)])
                            S.act(dls[x2][:], dls[x2][:], AF.Exp, reads=[bf("dls%d" % x2)], writes=[bf("dls%d" % x2)])
                            S.tt("dve", dl[x2][:], dls[x2][:], identF[:], ALU.add, reads=[bf("dls%d" % x2), b_const], writes=[bf("dl%d" % x2)])
                            for half in range(2):
                                r = slice(64 * half, 64 * half + 64)
                                lc = 64 * half + 63
                                S.tt("dve", gld[r, sub, hh:hh + 1], pG[r, lc:lc + 1], gcum[r, sub, hh:hh + 1], ALU.subtract,
                                     reads=[bf("gcum"), bf("pG")], writes=[bf("gld")])
                            S.act(gld[:, sub, hh:hh + 1], gld[:, sub, hh:hh + 1], AF.Exp, reads=[bf("gld")], writes=[bf("gld")])
                            pgr, bgr = pn_slot()
                            S.mm(pgr, KT[:, hh, tsl], KT[:, hh, tsl], reads=[bf("KT%d" % hh)], writes=[bgr])
                            pqk, bqk = pn_slot()
                            S.mm(pqk, QT[:, hh, tsl], KT[:, hh, tsl], reads=[bf("QT%d" % hh), bf("KT%d" % hh)], writes=[bqk])
                            S.stt("dve", Xs[0][:], pgr, nbeta[:, sub, hh:hh + 1], dls[x2][:], ALU.mult, ALU.mult,
                                  reads=[bgr, bf("nbeta"), bf("dls%d" % x2)], writes=[bf("X0")])
                            S.tt("dve", a_sb[:], pqk, dl[x2][:], ALU.mult, reads=[bqk, bf("dl%d" % x2)], writes=[bf("a_sb")])
                            pat, bat = pnb_slot()
                            S.tr(pat, a_sb[:], identB[:], reads=[bf("a_sb"), b_const], writes=[bat])
                            S.copy("act", aT[:], pat, reads=[bat], writes=[bf("aT")])
                            py, by = pn_slot()
                            S.tr(py, Xs[0][:], identF[:], reads=[bf("X0"), b_const], writes=[by])
                            S.copy("act", Ys[0][:], py, reads=[by], writes=[bf("Y0")])
                            S.tt("dve", TTs[0][:], py, identF[:], ALU.add, reads=[by, b_const], writes=[bf("TT0")])
                            cur = 0
                            for lvl in range(5):
                                nxt = 1 - cur
                                px, bx = pn_slot()
                                S.mm(px, Ys[cur][:], Xs[cur][:], reads=[bf("Y%d" % cur), bf("X%d" % cur)], writes=[bx])
                                S.copy("act", Xs[nxt][:], px, reads=[bx], writes=[bf("X%d" % nxt)])
                                if lvl < 4:
                                    py2, by2 = pn_slot()
                                    S.mm(py2, Xs[cur][:], Ys[cur][:], reads=[bf("Y%d" % cur), bf("X%d" % cur)], writes=[by2])
                                    S.copy("dve", Ys[nxt][:], py2, reads=[by2], writes=[bf("Y%d" % nxt)])
                                pt_, bt_ = pn_slot()
                                S.mm(pt_, Xs[nxt][:], TTs[cur][:], reads=[bf("X%d" % nxt), bf("TT%d" % cur)], writes=[bt_])
                                if lvl < 4:
                                    S.tt("dve", TTs[nxt][:], pt_, TTs[cur][:], ALU.add, reads=[bt_, bf("TT%d" % cur)], writes=[bf("TT%d" % nxt)])
                                else:
                                    S.tt("dve", TTb[:], pt_, TTs[cur][:], ALU.add, reads=[bt_, bf("TT%d" % cur)], writes=[bf("TTb")])
                                cur = nxt
                            pk_, bk_ = pnb_slot()
                            S.tr(pk_, KT[:, hh, tsl], identB[:], reads=[bf("KT%d" % hh), b_const], writes=[bk_])
                            S.ts("dve", Kbg[:], pk_, bge[:, sub, hh:hh + 1], ALU.mult, reads=[bk_, bf("bge")], writes=[bf("Kbg")])
                            S.act(kdec[:], pk_, AF.Identity, scale=gld[:, sub, hh:hh + 1], reads=[bk_, bf("gld")], writes=[bf("kdec")])
                            pv_, bv_ = pn_slot()
                            S.tr(pv_, VT[:, hh, tsl], identF[:], reads=[bf("VT%d" % hh), b_const], writes=[bv_])
                            S.act(Vb[:], pv_, AF.Identity, scale=beta[:, sub, hh:hh + 1], reads=[bv_, bf("beta")], writes=[bf("Vb")])
                            pu_, bu_ = pn_slot()
                            S.mm(pu_, TTb[:], Vb[:], reads=[bf("TTb"), bf("Vb")], writes=[bu_])
                            S.copy("act", u_sb[:], pu_, reads=[bu_], writes=[bf("u_sb")])
                            pw_, bw_ = pn_slot()
                            S.mm(pw_, Kbg[:], TTb[:], reads=[bf("TTb"), bf("Kbg")], writes=[bw_])
                            S.copy("dve", wT[:], pw_, reads=[bw_], writes=[bf("wT")])
                            S.tt("dve", QgT[:], QT[:, hh, tsl], EG[x2][:], ALU.mult, reads=[bf("QT%d" % hh), bf("EG%d" % x2)], writes=[bf("QgT")])
                            for half in range(2):
                                r = slice(64 * half, 64 * half + 64)
                                lc = 64 * half + 63
                                pws, bws = psc_slot()
                                vn = vnw[half]
                                bvn = bf("vnew%d" % half)
                                S.mm(pws, wT[:], Sb[:, hh, :], reads=[bf("wT"), bf("Sb_%d" % hh)], writes=[bws])
                                S.tt("dve", vn[r, :], u_sb[r, :], pws[r, :], ALU.subtract, reads=[bf("u_sb"), bws], writes=[bvn])
                                po_, bo_ = psc_slot()
                                S.mm(po_, QgT[:], Sb[:, hh, :], start=True, stop=False, reads=[bf("QgT"), bf("Sb_%d" % hh)], writes=[bo_])
                                S.mm(po_, aT[:], vn[:], start=False, stop=True, reads=[bf("aT"), bvn], writes=[bo_])
                                pds, bds = psc_slot()
                                S.mm(pds, kdec[:], vn[:], reads=[bf("kdec"), bvn], writes=[bds])
                                S.stt("dve", S32[:, hh, :], S32[:, hh, :], EG[x2][:, lc:lc + 1], pds, ALU.mult, ALU.add,
                                      reads=[bf("S32_%d" % hh), bf("EG%d" % x2), bds], writes=[bf("S32_%d" % hh)])
                                S.copy("act", Sb[:, hh, :], S32[:, hh, :], reads=[bf("S32_%d" % hh)], writes=[bf("Sb_%d" % hh)])
                                S.memset("dve", oss[r, :], 0.0, writes=[bf("oss")])
                                S.act(osq[r, :], po_[r, :], AF.Square, accum=oss[r, :], reads=[bo_, bf("oss")], writes=[bf("osq"), bf("oss")])
                                S.act(oss[r, :], oss[r, :], AF.Sqrt, bias=EPS, scale=1.0 / 128, reads=[bf("oss")], writes=[bf("oss")])
                                S.op("dve", (lambda r=r: (lambda e: e.reciprocal(out=oss[r, :], in_=oss[r, :])))(), reads=[bf("oss")], writes=[bf("oss")])
                                S.stt("dve", on[gi2][r, hh * 128:(hh + 1) * 128], po_[r, :], oss[r, :], gws[gi2][r, hh * 128:(hh + 1) * 128],
                                      ALU.mult, ALU.mult, reads=[bo_, bf("oss"), bf("gws%d" % gi2)], writes=[bf("on%d" % gi2)])
                        for hh in range(H):
                            pt2, bt2 = pnb_slot()
                            S.tr(pt2, on[gi2][:, hh * 128:(hh + 1) * 128], identB[:], reads=[bf("on%d" % gi2), b_const], writes=[bt2])
                            S.copy("act", onT[sb2][:, hh, tsl], pt2, reads=[bt2], writes=[bf("onT%d" % sb2)])
                    S.dma("sp", mixT_d[gi * 8 + 4:gi * 8 + 8, :, st * 512:(st + 1) * 512].rearrange("k p t -> p k t"), onT[sb2][:],
                          ds_o[sb2], reads=[bf("onT%d" % sb2)], writes=[b_mixT_parts[gi][1]])
                S.barrier()
                S.emit(block)

        b_mixT_parts = [[Buf(), Buf()] for _ in range(NG)]
        if not skip_mixer:
            for gi in range(NG):
                attn_pass(gi)
                if not cfg.get("skip_dn"):
                    dn_pass(gi)

        with ExitStack() as ph, nc.Block() as block:
            T = alloc_ffn(ph)
            woutv = wout_d.rearrange("(mc p) o -> p mc o", p=128)
            for g in range(NGRP):
                t0, t1 = g * TG, (g + 1) * TG
                S.dmas("sp", [(T["acc"][:, 4 * q:4 * q + 4, :], h1T_d[4 * q:4 * q + 4, :, t0:t1].rearrange("k p t -> p k t")) for q in range(4)],
                       T["ds_ld"][0], reads=[b_h1T[g]], writes=[T["b_acc"][kc][tt] for kc in range(KC) for tt in range(NTT)])
                if not skip_mixer:
                    S.dmas("sp", [(T["uT"][:, 4 * q:4 * q + 4, :],
                                   mixT_d[4 * q:4 * q + 4, :, TOK0 + t0:TOK0 + t1].rearrange("k p t -> p k t")) for q in range(4)],
                           T["ds_ld"][1], reads=[b_mixT], writes=[T["b_uT"][tt] for tt in range(NTT)])
                    for ob in range(D // FB):
                        s = ob % 2
                        for h in range(2):
                            S.dma("pool", T["wgb"][s][:, 8 * h:8 * h + 8, :], woutv[:, 8 * h:8 * h + 8, ob * FB:(ob + 1) * FB],
                                  T["ds_wg"][s][h], writes=[T["b_wg"][s][h]])
                        for oc2 in range(FB // 128):
                            oc = ob * (FB // 128) + oc2
                            for tt in range(NTT):
                                i = T["cnt"]["pd"] % 2
                                T["cnt"]["pd"] += 1
                                for mc in range(KC):
                                    S.mm(T["pd"][i][:], T["wgb"][s][:, mc, oc2 * 128:(oc2 + 1) * 128],
                                         T["uT"][:, mc, tt * 512:(tt + 1) * 512], start=(mc == 0), stop=(mc == KC - 1),
                                         reads=[T["b_wg"][s][mc // 8], T["b_uT"][tt]], writes=[T["b_pd"][i]])
                                a = T["acc"][:, oc, tt * 512:(tt + 1) * 512]
                                S.stt("dve", a, T["pd"][i][:], vec[:, G2, oc:oc + 1], a, ALU.mult, ALU.add,
                                      reads=[T["b_pd"][i], T["b_acc"][oc][tt], b_vec], writes=[T["b_acc"][oc][tt]])
                norm_group(T, GW3, SH3)
                ffn_group(T, wg_d[1], wu_d[1], wd_d[1], G3H)
                for tt in range(NTT):
                    rms_stats(T, tt)
                    for kc in range(KC):
                        a = T["acc"][:, kc, tt * 512:(tt + 1) * 512]
                        S.stt("dve", a, a, vec[:, FW, kc:kc + 1], T["rstd"][:], ALU.mult, ALU.mult,
                              reads=[T["b_acc"][kc][tt], T["b_rstd"], b_vec], writes=[T["b_acc"][kc][tt]])
                    for sub in range(4):
                        i = T["cnt"]["xin"] % 2
                        T["cnt"]["xin"] += 1
                        c0 = tt * 512 + sub * 128
                        for q in range(4):
                            for j in range(4):
                                kc = 4 * q + j
                                S.tr(T["pt"][:, j, :], T["acc"][:, kc, c0:c0 + 128], identF[:],
                                     reads=[T["b_acc"][kc][tt], b_const], writes=[T["b_pt"]])
                            S.copy("act" if q % 2 else "dve", T["xin"][i][:, q * 512:(q + 1) * 512],
                                   T["pt"][:].rearrange("p j d -> p (j d)"),
                                   reads=[T["b_pt"]], writes=[T["b_xin"][i]])
                        r0 = t0 + c0
                        S.dma("sp", out_d[r0:r0 + 128, :], T["xin"][i][:], T["ds_io"][i], reads=[T["b_xin"][i]])
            S.barrier()
            S.emit(block)
    return nc


def col_layout(v):
    v = np.asarray(v)
    return np.ascontiguousarray(v.reshape(-1, 128).T)


def make_in_map(inp, b, tok_lo, tok_hi, groups=(0, 1)):
    m = {}
    m["x"] = np.ascontiguousarray(inp["x"][b, tok_lo:tok_hi])
    m["ccol"] = col_layout(inp["c"][b])
    m["ada_w"] = np.ascontiguousarray(inp["ada_w"][0])
    m["adab_col"] = col_layout(inp["ada_b"][0])
    m["ncol"] = np.ascontiguousarray(np.stack([col_layout(inp["norm_ffn1"][0]), col_layout(inp["norm_mix"][0]),
                                               col_layout(inp["norm_ffn2"][0]), col_layout(inp["final_norm"])], axis=1))
    m["wg1"] = np.ascontiguousarray(inp["ffn1_w_gate"][0])
    m["wu1"] = np.ascontiguousarray(inp["ffn1_w_up"][0])
    m["wd1"] = np.ascontiguousarray(inp["ffn1_w_down"][0])
    m["wg2"] = np.ascontiguousarray(inp["ffn2_w_gate"][0])
    m["wu2"] = np.ascontiguousarray(inp["ffn2_w_up"][0])
    m["wd2"] = np.ascontiguousarray(inp["ffn2_w_down"][0])
    wout = inp["w_out"][0]
    w_in = inp["w_in"][0]
    conv_w = inp["conv_w"][0]
    rows = []
    for j, G in enumerate(groups):
        rows += [wout[512 * G:512 * G + 512], wout[1024 + 512 * G:1024 + 512 * G + 512]]
        m["watt%d" % j] = np.ascontiguousarray(np.concatenate(
            [w_in[:, 512 * G:512 * G + 512], w_in[:, 1024 + 64 * G:1024 + 64 * G + 64], w_in[:, 1152 + 64 * G:1152 + 64 * G + 64]], axis=1))
        m["wdn%d" % j] = np.ascontiguousarray(np.concatenate(
            [w_in[:, 1280 + 512 * G:1280 + 512 * G + 512], w_in[:, 2304 + 512 * G:2304 + 512 * G + 512],
             w_in[:, 3328 + 512 * G:3328 + 512 * G + 512]], axis=1))
        m["wgate%d" % j] = np.ascontiguousarray(w_in[:, 4352 + 512 * G:4352 + 512 * G + 512])
        m["wab%d" % j] = np.ascontiguousarray(np.concatenate(
            [w_in[:, 5376 + 4 * G:5376 + 4 * G + 4], w_in[:, 5384 + 4 * G:5384 + 4 * G + 4]], axis=1))
        cw = np.zeros((128, 12, 4), np.float32)
        for c in range(12):
            typ, hh = c // 4, 4 * G + c % 4
            ch0 = typ * 1024 + hh * 128
            cw[:, c, :] = conv_w[:, ch0:ch0 + 128].T
        m["convw%d" % j] = cw
        m["hrow%d" % j] = np.ascontiguousarray(np.concatenate(
            [inp["a_log"][0, 4 * G:4 * G + 4], inp["dt_bias"][0, 4 * G:4 * G + 4], inp["attn_sinks"][0, 8 * G:8 * G + 8]])[None, :]).astype(np.float32)
    other = [G for G in (0, 1) if G not in groups]
    for G in other:
        rows += [wout[512 * G:512 * G + 512], wout[1024 + 512 * G:1024 + 512 * G + 512]]
    m["wout"] = np.ascontiguousarray(np.concatenate(rows, axis=0))
    m["dnw"] = np.ascontiguousarray(inp["dn_norm_w"][0][None, :])
    m["pos"] = np.ascontiguousarray(inp["positions"][b].reshape(32, 128).T.astype(np.int32))
    return m


CFG = {"NT": 4096, "TG": 1024, "NG": 2, "TOK0": 0}


def kernel(**inputs):
    inp = {k: np.asarray(v) for k, v in inputs.items()}
    nc = build_program(CFG)
    in_maps = [make_in_map(inp, i % 4, 0, 4096) for i in range(8)]
    res = run_bass_kernel_spmd(nc, in_maps, core_ids=list(range(8)))
    out = np.stack([np.asarray(res.results[b]["out"]) for b in range(4)], axis=0)
    return out.astype(np.float32)
```

```python
import numpy as np
from contextlib import ExitStack
import concourse.bass as bass
import concourse.mybir as mybir
from concourse.bass_utils import run_bass_kernel_spmd

F32 = mybir.dt.float32
BF16 = mybir.dt.bfloat16
I32 = mybir.dt.int32
AF = mybir.ActivationFunctionType
ALU = mybir.AluOpType
AX = mybir.AxisListType

ENGS = ("pe", "act", "dve", "pool", "sp")

D = 2048
DFF = 5632
NS = 4096
KC = 16
FB = 256
NFB = DFF // FB
NMOD = 9
EPS = 1e-6
TWO_PI = float(2 * np.pi)
import os as _osenv
SELFSYNC = _osenv.environ.get("SELFSYNC", "1") == "1"


class Buf:
    __slots__ = ("name", "w", "r", "excl")

    def __init__(self, name="", excl=False):
        self.name = name
        self.w = None
        self.r = []
        self.excl = excl


class DmaSem:
    __slots__ = ("sem", "val")

    def __init__(self, sem):
        self.sem = sem
        self.val = 0


class Sched:
    def __init__(self, nc, sems, es):
        self.nc = nc
        self.es = es
        self.sems = sems
        self.count = {e: 0 for e in ENGS}
        self.prog = {e: [] for e in ENGS}
        self.seen = {e: {} for e in ENGS}
        self.semobj = {}
        self.dsems = []
        for e in ENGS:
            self.semobj[("e", e)] = sems[e]

    def new_dsem(self, name=None):
        d = DmaSem(self.es.enter_context(self.nc.semaphore(name or ("dq%d" % len(self.dsems)))))
        self.dsems.append(d)
        self.semobj[("d", id(d))] = d.sem
        return d

    def _need(self, eng, tok, waits):
        if tok is None:
            return
        key, val = tok
        if key == ("e", eng) and (eng == "pe" or not SELFSYNC):
            return
        if self.seen[eng].get(key, 0) >= val:
            return
        self.seen[eng][key] = val
        waits.append((key, val))

    def _deps(self, eng, reads, writes, extra):
        waits = []
        for b in reads:
            self._need(eng, b.w, waits)
        for b in writes:
            self._need(eng, b.w, waits)
            for t in b.r:
                self._need(eng, t, waits)
        for t in extra:
            self._need(eng, t, waits)
        return waits

    def _post(self, tok, reads, writes):
        for b in reads:
            b.r.append(tok)
        for b in writes:
            b.w = tok
            b.r = []

    def op(self, eng, fn, reads=(), writes=(), extra=()):
        ex = [b for b in reads if b.excl]
        if ex:
            writes = list(writes) + [b for b in ex if b not in writes]
        waits = self._deps(eng, reads, writes, extra)
        self.count[eng] += 1
        tok = (("e", eng), self.count[eng])
        self.prog[eng].append((waits, fn, None))
        self._post(tok, reads, writes)
        return tok

    def dma(self, q, out, in_, dsem, reads=(), writes=(), extra=()):
        waits = self._deps(q, reads, writes, extra)
        dsem.val += 16
        tok = (("d", id(dsem)), dsem.val)
        self.prog[q].append((waits, (lambda e: e.dma_start(out=out, in_=in_)), dsem))
        self._post(tok, reads, writes)
        return tok

    def dmas(self, q, pairs, dsem, reads=(), writes=(), extra=()):
        waits = self._deps(q, reads, writes, extra)
        for k, (o, i) in enumerate(pairs):
            dsem.val += 16
            self.prog[q].append((waits if k == 0 else [], (lambda e, o=o, i=i: e.dma_start(out=o, in_=i)), dsem))
        tok = (("d", id(dsem)), dsem.val)
        self._post(tok, reads, writes)
        return tok

    def barrier(self):
        toks = [(("e", e), self.count[e]) for e in ENGS if self.count[e] > 0]
        toks += [(("d", id(d)), d.val) for d in self.dsems if d.val > 0]
        for e in ENGS:
            waits = []
            for t in toks:
                if t[0] == ("e", e):
                    continue
                self._need(e, t, waits)
            self.prog[e].append((waits, None, None))

    def emit(self, block):
        me = self
        progs = self.prog
        self.prog = {e: [] for e in ENGS}

        def run(name, engine):
            for waits, fn, dsem in progs[name]:
                for key, val in waits:
                    engine.wait_ge(me.semobj[key], val)
                if fn is None:
                    continue
                inst = fn(engine)
                if dsem is not None:
                    inst.then_inc(dsem.sem, 16)
                else:
                    inst.then_inc(me.sems[name], 1)

        @block.tensor
        def _(e):
            run("pe", e)

        @block.scalar
        def _(e):
            run("act", e)

        @block.vector
        def _(e):
            run("dve", e)

        @block.gpsimd
        def _(e):
            run("pool", e)

        @block.sync
        def _(e):
            run("sp", e)

    def mm(self, out, lhsT, rhs, start=True, stop=True, reads=(), writes=()):
        return self.op("pe", lambda e: e.matmul(out, lhsT=lhsT, rhs=rhs, start=start, stop=stop), reads, writes)

    def tr(self, out, in_, ident, reads=(), writes=()):
        return self.op("pe", lambda e: e.transpose(out, in_, ident), reads, writes)

    def act(self, out, in_, func, bias=None, scale=None, accum=None, reads=(), writes=()):
        kw = {}
        if bias is not None:
            kw["bias"] = bias
        if scale is not None:
            kw["scale"] = scale
        if accum is not None:
            kw["accum_out"] = accum
        return self.op("act", lambda e: e.activation(out=out, in_=in_, func=func, **kw), reads, writes)

    def copy(self, eng, out, in_, reads=(), writes=()):
        if eng == "act":
            return self.op("act", lambda e: e.activation(out=out, in_=in_, func=AF.Identity), reads, writes)
        return self.op(eng, lambda e: e.tensor_copy(out=out, in_=in_), reads, writes)

    def tt(self, eng, out, in0, in1, op, reads=(), writes=()):
        return self.op(eng, lambda e: e.tensor_tensor(out=out, in0=in0, in1=in1, op=op), reads, writes)

    def ts(self, eng, out, in0, s1, op0, s2=None, op1=None, accum=None, reads=(), writes=()):
        kw = {}
        if op1 is not None:
            kw["op1"] = op1
        if accum is not None:
            kw["accum_out"] = accum
        return self.op(eng, lambda e: e.tensor_scalar(out=out, in0=in0, scalar1=s1, scalar2=s2, op0=op0, **kw), reads, writes)

    def stt(self, eng, out, in0, scalar, in1, op0, op1, reads=(), writes=()):
        return self.op(eng, lambda e: e.scalar_tensor_tensor(out=out, in0=in0, scalar=scalar, in1=in1, op0=op0, op1=op1),
                       reads, writes)

    def memset(self, eng, ap, val, writes=()):
        return self.op(eng, lambda e: e.memset(ap, val), (), writes)


def build_program(cfg):
    NT = cfg["NT"]
    TG = cfg["TG"]
    NG = cfg["NG"]
    TOK0 = cfg.get("TOK0", 0)
    skip_mixer = cfg.get("skip_mixer", False)
    dbg = cfg.get("dbg", ())
    NGRP = 0 if cfg.get("mixer_test") else NT // TG
    NTT = TG // 512
    NSUB = TG // 128

    nc = bass.Bass("TRN2", target_bir_lowering=False)

    def din(name, shape, dt=F32):
        return nc.dram_tensor(name, list(shape), dt, kind="ExternalInput").ap()

    def dint(name, shape, dt=F32):
        return nc.dram_tensor(name, list(shape), dt, kind="Internal").ap()

    x_d = din("x", [NT, D])
    ccol_d = din("ccol", [128, KC])
    adaw_d = din("ada_w", [D, NMOD * D])
    adab_d = din("adab_col", [128, NMOD * KC])
    ncol_d = din("ncol", [128, 4, KC])
    wg_d = [din("wg%d" % i, [D, DFF]) for i in (1, 2)]
    wu_d = [din("wu%d" % i, [D, DFF]) for i in (1, 2)]
    wd_d = [din("wd%d" % i, [DFF, D]) for i in (1, 2)]
    wout_d = din("wout", [D, D])
    out_d = nc.dram_tensor("out", [NT, D], F32, kind="ExternalOutput").ap()
    h1T_d = dint("h1T", [KC, 128, NT])
    if cfg.get("mixer_test"):
        u2T_d = din("u2T_in", [KC, 128, NS], BF16)
        mixT_d = nc.dram_tensor("mixT_out", [KC, 128, NS], BF16, kind="ExternalOutput").ap()
    else:
        u2T_d = dint("u2T", [KC, 128, NS], BF16)
        mixT_d = dint("mixT", [KC, 128, NS], BF16)
    watt_d = [din("watt%d" % g, [D, 640]) for g in range(NG)]
    wdn_d = [din("wdn%d" % g, [D, 1536]) for g in range(NG)]
    wgate_d = [din("wgate%d" % g, [D, 512]) for g in range(NG)]
    wab_d = [din("wab%d" % g, [D, 8]) for g in range(NG)]
    convw_d = [din("convw%d" % g, [128, 12, 4]) for g in range(NG)]
    hrow_d = [din("hrow%d" % g, [1, 16]) for g in range(NG)]
    dnw_d = din("dnw", [1, 128])
    pos_d = din("pos", [128, 32], I32)
    dbg_d = {}
    for nm, shp in dbg:
        dbg_d[nm] = nc.dram_tensor("dbg_" + nm, list(shp), F32, kind="ExternalOutput").ap()

    with ExitStack() as es:
        sems = {e: es.enter_context(nc.semaphore("s_" + e)) for e in ENGS}
        S = Sched(nc, sems, es)

        uniq = [0]

        def sbuf(scope, name, shape, dt):
            uniq[0] += 1
            return scope.enter_context(nc.sbuf_tensor("%s_%d" % (name, uniq[0]), list(shape), dt))

        def psum(scope, name, shape, dt=F32):
            uniq[0] += 1
            return scope.enter_context(nc.psum_tensor("%s_%d" % (name, uniq[0]), list(shape), dt))

        identF = sbuf(es, "identF", [128, 128], F32)
        identB = sbuf(es, "identB", [128, 128], BF16)
        onesB = sbuf(es, "onesB", [128, 128], BF16)
        vec = sbuf(es, "vec", [128, 10, KC], F32)
        b_const = Buf("const")
        b_vec = Buf("vec")
        GW1, SH1, G1H, GW2, SH2, G2, GW3, SH3, G3H, FW = range(10)

        b_h1T = [Buf("h1T%d" % g) for g in range(NGRP)]
        b_u2T = [Buf("u2T%d" % g) for g in range(NGRP)]
        b_mixT = Buf("mixT")

        with ExitStack() as ph, nc.Block() as block:
            idi = sbuf(ph, "idi", [128, 128], I32)
            ccol = sbuf(ph, "ccol_s", [128, KC], F32)
            cact = sbuf(ph, "cact", [128, KC], BF16)
            adab = sbuf(ph, "adab_s", [128, NMOD * KC], F32)
            ncol = sbuf(ph, "ncol_s", [128, 4, KC], F32)
            modc = sbuf(ph, "modc", [128, NMOD * KC], F32)
            JB = 512
            NJB = 0 if cfg.get("mixer_test") else NMOD * D // JB
            awb = [sbuf(ph, "awb%d" % i, [128, KC, JB], BF16) for i in range(2)]
            pmod = psum(ph, "pmod", [128, 512])
            b_idi, b_ccol, b_cact, b_adab, b_ncol, b_modc, b_pmod = (Buf() for _ in range(7))
            b_aw = [[Buf(), Buf()] for _ in range(2)]
            ds_small = [S.new_dsem() for _ in range(4)]
            ds_aw = [[S.new_dsem() for _ in range(2)] for _ in range(2)]

            S.op("pool", lambda e: e.iota(idi[:], pattern=[[1, 128]], base=0, channel_multiplier=-1), writes=[b_idi])
            S.ts("dve", identF[:], idi[:], 0, ALU.is_equal, reads=[b_idi], writes=[b_const])
            S.copy("dve", identB[:], identF[:], reads=[b_const], writes=[b_const])
            S.memset("dve", onesB[:], 1.0, writes=[b_const])
            S.dma("sp", ccol[:], ccol_d, ds_small[0], writes=[b_ccol])
            S.dma("sp", adab[:], adab_d, ds_small[1], writes=[b_adab])
            S.dma("sp", ncol[:], ncol_d, ds_small[2], writes=[b_ncol])
            S.act(cact[:], ccol[:], AF.Silu, reads=[b_ccol], writes=[b_cact])
            awv = adaw_d.rearrange("(kc p) j -> p kc j", p=128)
            for jb in range(NJB):
                s = jb % 2
                for h in range(2):
                    S.dma("pool", awb[s][:, 8 * h:8 * h + 8, :], awv[:, 8 * h:8 * h + 8, jb * JB:(jb + 1) * JB], ds_aw[s][h],
                          writes=[b_aw[s][h]])
                for jc in range(JB // 128):
                    col = jb * (JB // 128) + jc
                    for kc in range(KC):
                        S.mm(pmod[:, col:col + 1], awb[s][:, kc, jc * 128:(jc + 1) * 128], cact[:, kc:kc + 1],
                             start=(kc == 0), stop=(kc == KC - 1), reads=[b_aw[s][kc // 8], b_cact], writes=[b_pmod])
            if NJB:
                S.tt("dve", modc[:], pmod[:, 0:NMOD * KC], adab[:], ALU.add, reads=[b_pmod, b_adab], writes=[b_modc])

                def msl(i):
                    return modc[:, i * KC:(i + 1) * KC]
                for (gw, sh, gg, half, ni, base) in ((GW1, SH1, G1H, 0.5, 0, 0), (GW2, SH2, G2, 1.0, 1, 3), (GW3, SH3, G3H, 0.5, 2, 6)):
                    S.stt("dve", vec[:, gw, :], msl(base + 1), 1.0, ncol[:, ni, :], ALU.add, ALU.mult,
                          reads=[b_modc, b_ncol], writes=[b_vec])
                    S.copy("dve", vec[:, sh, :], msl(base), reads=[b_modc], writes=[b_vec])
                    S.ts("dve", vec[:, gg, :], msl(base + 2), half, ALU.mult, reads=[b_modc], writes=[b_vec])
                S.copy("dve", vec[:, FW, :], ncol[:, 3, :], reads=[b_ncol], writes=[b_vec])
            if "vec" in dbg_d:
                S.dma("sp", dbg_d["vec"], vec[:].rearrange("p a k -> p (a k)"), ds_small[3], reads=[b_vec])
            S.barrier()
            S.emit(block)

        def alloc_ffn(ph):
            T = {}
            T["acc"] = sbuf(ph, "acc", [128, KC, TG], F32)
            T["uT"] = sbuf(ph, "uT", [128, KC, TG], BF16)
            T["wgb"] = [sbuf(ph, "wgb%d" % i, [128, KC, FB], BF16) for i in range(2)]
            T["wub"] = [sbuf(ph, "wub%d" % i, [128, KC, FB], BF16) for i in range(2)]
            T["wdb"] = [sbuf(ph, "wdb%d" % i, [128, FB // 128, D], BF16) for i in range(2)]
            T["actT"] = [sbuf(ph, "actT%d" % i, [128, FB // 128, TG], BF16) for i in range(2)]
            T["sq"] = [sbuf(ph, "sq%d" % i, [128, 512], BF16) for i in range(2)]
            T["rstd"] = sbuf(ph, "rstd", [128, 512], F32)
            T["tmpn"] = [sbuf(ph, "tmpn%d" % i, [128, 512], F32) for i in range(2)]
            T["sil"] = [sbuf(ph, "sil%d" % i, [128, 512], F32) for i in range(2)]
            T["xin"] = [sbuf(ph, "xin%d" % i, [128, D], F32) for i in range(2)]
            T["pg"] = [psum(ph, "pg%d" % i, [128, 512]) for i in range(2)]
            T["pu"] = [psum(ph, "pu%d" % i, [128, 512]) for i in range(2)]
            T["pd"] = [psum(ph, "pd%d" % i, [128, 512]) for i in range(2)]
            T["pm"] = psum(ph, "pm", [128, 512])
            T["pt"] = psum(ph, "pt", [128, 4, 128])
            T["b_acc"] = [[Buf() for _ in range(NTT)] for _ in range(KC)]
            T["b_uT"] = [Buf() for _ in range(NTT)]
            T["b_wg"] = [[Buf(), Buf()] for _ in range(2)]
            T["b_wu"] = [[Buf(), Buf()] for _ in range(2)]
            T["b_wd"] = [[Buf() for _ in range(FB // 128)] for _ in range(2)]
            T["b_actT"] = [[[Buf() for _ in range(NTT)] for _ in range(FB // 128)] for _ in range(2)]
            T["b_sq"] = [Buf(), Buf()]
            T["b_rstd"] = Buf()
            T["b_tmpn"] = [Buf(), Buf()]
            T["b_sil"] = [Buf(), Buf()]
            T["b_xin"] = [Buf(), Buf()]
            T["b_pg"] = [Buf(), Buf()]
            T["b_pu"] = [Buf(), Buf()]
            T["b_pd"] = [Buf(), Buf()]
            T["b_pm"] = Buf()
            T["b_pt"] = Buf()
            T["ds_wg"] = [[S.new_dsem() for _ in range(2)] for _ in range(2)]
            T["ds_wu"] = [[S.new_dsem() for _ in range(2)] for _ in range(2)]
            T["ds_wd"] = [[S.new_dsem() for _ in range(FB // 128)] for _ in range(2)]
            T["ds_io"] = [S.new_dsem() for _ in range(2)]
            T["ds_st"] = [S.new_dsem() for _ in range(2)]
            T["ds_ld"] = [S.new_dsem() for _ in range(2)]
            T["cnt"] = {"gu": 0, "pd": 0, "sq": 0, "tmpn": 0, "xin": 0}
            return T

        def rms_stats(T, tt):
            acc = T["acc"]
            for kc in range(KC):
                i = T["cnt"]["sq"] % 2
                T["cnt"]["sq"] += 1
                S.act(T["sq"][i][:], acc[:, kc, tt * 512:(tt + 1) * 512], AF.Square,
                      reads=[T["b_acc"][kc][tt]], writes=[T["b_sq"][i]])
                S.mm(T["pm"][:], onesB[:], T["sq"][i][:], start=(kc == 0), stop=(kc == KC - 1),
                     reads=[T["b_sq"][i], b_const], writes=[T["b_pm"]])
            S.act(T["rstd"][:], T["pm"][:], AF.Sqrt, bias=EPS, scale=1.0 / D, reads=[T["b_pm"]], writes=[T["b_rstd"]])
            S.op("dve", lambda e: e.reciprocal(out=T["rstd"][:], in_=T["rstd"][:]), reads=[T["b_rstd"]], writes=[T["b_rstd"]])

        def norm_group(T, gw, sh):
            acc, uT = T["acc"], T["uT"]
            for tt in range(NTT):
                rms_stats(T, tt)
                for kc in range(KC):
                    i = T["cnt"]["tmpn"] % 2
                    T["cnt"]["tmpn"] += 1
                    S.tt("dve", T["tmpn"][i][:], acc[:, kc, tt * 512:(tt + 1) * 512], T["rstd"][:], ALU.mult,
                         reads=[T["b_acc"][kc][tt], T["b_rstd"]], writes=[T["b_tmpn"][i]])
                    S.act(uT[:, kc, tt * 512:(tt + 1) * 512], T["tmpn"][i][:], AF.Identity,
                          bias=vec[:, sh, kc:kc + 1], scale=vec[:, gw, kc:kc + 1],
                          reads=[T["b_tmpn"][i], b_vec], writes=[T["b_uT"][tt]])

        def ffn_group(T, wg, wu, wd, gate):
            acc, uT = T["acc"], T["uT"]
            wgv = wg.rearrange("(kc p) f -> p kc f", p=128)
            wuv = wu.rearrange("(kc p) f -> p kc f", p=128)
            wdv = wd.rearrange("(fc p) o -> p fc o", p=128)
            NFC = FB // 128
            for fb in range(NFB):
                s = fb % 2
                for h in range(2):
                    S.dma("pool", T["wgb"][s][:, 8 * h:8 * h + 8, :], wgv[:, 8 * h:8 * h + 8, fb * FB:(fb + 1) * FB],
                          T["ds_wg"][s][h], writes=[T["b_wg"][s][h]])
                for h in range(2):
                    S.dma("pool", T["wub"][s][:, 8 * h:8 * h + 8, :], wuv[:, 8 * h:8 * h + 8, fb * FB:(fb + 1) * FB],
                          T["ds_wu"][s][h], writes=[T["b_wu"][s][h]])
                for fc in range(NFC):
                    S.dma("pool", T["wdb"][s][:, fc, :], wdv[:, fb * NFC + fc, :], T["ds_wd"][s][fc], writes=[T["b_wd"][s][fc]])
                for fc in range(NFC):
                    for tt in range(NTT):
                        i = T["cnt"]["gu"] % 2
                        T["cnt"]["gu"] += 1
                        for kc in range(KC):
                            S.mm(T["pg"][i][:], T["wgb"][s][:, kc, fc * 128:(fc + 1) * 128], uT[:, kc, tt * 512:(tt + 1) * 512],
                                 start=(kc == 0), stop=(kc == KC - 1),
                                 reads=[T["b_wg"][s][kc // 8], T["b_uT"][tt]], writes=[T["b_pg"][i]])
                        for kc in range(KC):
                            S.mm(T["pu"][i][:], T["wub"][s][:, kc, fc * 128:(fc + 1) * 128], uT[:, kc, tt * 512:(tt + 1) * 512],
                                 start=(kc == 0), stop=(kc == KC - 1),
                                 reads=[T["b_wu"][s][kc // 8], T["b_uT"][tt]], writes=[T["b_pu"][i]])
                        S.act(T["sil"][i][:], T["pg"][i][:], AF.Silu, reads=[T["b_pg"][i]], writes=[T["b_sil"][i]])
                        S.tt("dve", T["actT"][s][:, fc, tt * 512:(tt + 1) * 512], T["sil"][i][:], T["pu"][i][:], ALU.mult,
                             reads=[T["b_sil"][i], T["b_pu"][i]], writes=[T["b_actT"][s][fc][tt]])
                for oc in range(KC):
                    for tt in range(NTT):
                        i = T["cnt"]["pd"] % 2
                        T["cnt"]["pd"] += 1
                        for fc in range(NFC):
                            S.mm(T["pd"][i][:], T["wdb"][s][:, fc, oc * 128:(oc + 1) * 128],
                                 T["actT"][s][:, fc, tt * 512:(tt + 1) * 512],
                                 start=(fc == 0), stop=(fc == NFC - 1),
                                 reads=[T["b_wd"][s][fc], T["b_actT"][s][fc][tt]], writes=[T["b_pd"][i]])
                        a = acc[:, oc, tt * 512:(tt + 1) * 512]
                        S.stt("dve", a, T["pd"][i][:], vec[:, gate, oc:oc + 1], a, ALU.mult, ALU.add,
                              reads=[T["b_pd"][i], T["b_acc"][oc][tt], b_vec], writes=[T["b_acc"][oc][tt]])

        def store_featmajor(T, dst, g, dt_bufs, src_key, src_bufs_fn, dsem):
            src = T[src_key]
            rd = []
            for kc in range(KC):
                rd += src_bufs_fn(kc)
            pairs = [(dst[4 * q:4 * q + 4, :, g[0]:g[1]].rearrange("k p t -> p k t"), src[:, 4 * q:4 * q + 4, :]) for q in range(4)]
            return S.dmas("sp", pairs, dsem, reads=rd, writes=dt_bufs)

        with ExitStack() as ph, nc.Block() as block:
            T = alloc_ffn(ph)
            for g in range(NGRP):
                for ts_ in range(NSUB):
                    i = T["cnt"]["xin"] % 2
                    T["cnt"]["xin"] += 1
                    r0 = g * TG + ts_ * 128
                    S.dma("sp", T["xin"][i][:], x_d[r0:r0 + 128, :], T["ds_io"][i], writes=[T["b_xin"][i]])
                    for q in range(4):
                        for j in range(4):
                            kc = 4 * q + j
                            S.tr(T["pt"][:, j, :], T["xin"][i][:, kc * 128:(kc + 1) * 128], identF[:],
                                 reads=[T["b_xin"][i], b_const], writes=[T["b_pt"]])
                        S.copy("act" if q % 2 else "dve", T["acc"][:, 4 * q:4 * q + 4, ts_ * 128:(ts_ + 1) * 128], T["pt"][:],
                               reads=[T["b_pt"]], writes=[T["b_acc"][kc][ts_ // 4] for kc in range(4 * q, 4 * q + 4)])
                norm_group(T, GW1, SH1)
                ffn_group(T, wg_d[0], wu_d[0], wd_d[0], G1H)
                store_featmajor(T, h1T_d, (g * TG, (g + 1) * TG), [b_h1T[g]], "acc",
                                lambda kc: [T["b_acc"][kc][tt] for tt in range(NTT)], T["ds_st"][0])
                norm_group(T, GW2, SH2)
                store_featmajor(T, u2T_d, (TOK0 + g * TG, TOK0 + (g + 1) * TG), [b_u2T[g]], "uT",
                                lambda kc: [T["b_uT"][tt] for tt in range(NTT)], T["ds_st"][1])
            S.barrier()
            S.emit(block)

        def u2_view(st):
            return u2T_d[:, :, st * 512:(st + 1) * 512].rearrange("k p t -> p k t")

        def build_rope_tables(ph, cosT, sinT, b_tab):
            posi = sbuf(ph, "posi", [128, 32], I32)
            posf = sbuf(ph, "posf", [128, 32], F32)
            invf = sbuf(ph, "invf", [128, 8], F32)
            ang = sbuf(ph, "ang", [128, 32, 8], F32)
            tf = sbuf(ph, "tf", [128, 32, 8], F32)
            ti = sbuf(ph, "ti", [128, 32, 8], I32)
            b_p, b_if, b_ang, b_tf, b_ti = Buf(), Buf(), Buf(), Buf(), Buf()
            ds = S.new_dsem()
            S.dma("sp", posi[:], pos_d, ds, writes=[b_p])
            S.copy("dve", posf[:], posi[:], reads=[b_p], writes=[b_p])
            for j in range(8):
                S.memset("dve", invf[:, j:j + 1], float(np.float32(500000.0 ** (-j / 8.0))), writes=[b_if])
            for which, dst in ((0, sinT), (1, cosT)):
                S.tt("dve", ang[:], posf[:].unsqueeze(2).to_broadcast([128, 32, 8]),
                     invf[:].unsqueeze(1).to_broadcast([128, 32, 8]), ALU.mult, reads=[b_p, b_if], writes=[b_ang])
                if which == 1:
                    S.ts("dve", ang[:], ang[:], float(np.pi / 2), ALU.add, reads=[b_ang], writes=[b_ang])
                S.ts("dve", tf[:], ang[:], 1.0 / TWO_PI, ALU.mult, reads=[b_ang], writes=[b_tf])
                S.copy("dve", ti[:], tf[:], reads=[b_tf], writes=[b_ti])
                S.copy("dve", tf[:], ti[:], reads=[b_ti], writes=[b_tf])
                S.stt("dve", ang[:], tf[:], -TWO_PI, ang[:], ALU.mult, ALU.add, reads=[b_tf, b_ang], writes=[b_ang])
                S.ts("dve", tf[:], ang[:], float(np.pi), ALU.is_gt, s2=-TWO_PI, op1=ALU.mult, reads=[b_ang], writes=[b_tf])
                S.tt("dve", ang[:], ang[:], tf[:], ALU.add, reads=[b_ang, b_tf], writes=[b_ang])
                S.ts("dve", ang[:], ang[:], float(np.pi), ALU.min, s2=-float(np.pi), op1=ALU.max, reads=[b_ang], writes=[b_ang])
                S.act(dst[:], ang[:], AF.Sin, reads=[b_ang], writes=[b_tab])

        def attn_pass(gi):
            with ExitStack() as ph, nc.Block() as block:
                watt = sbuf(ph, "watt", [128, KC, 640], BF16)
                u2s = [sbuf(ph, "u2s%d" % i, [128, KC, 512], BF16) for i in range(2)]
                kT2 = sbuf(ph, "kT2", [128, 33 * 128], BF16)
                Vall = sbuf(ph, "Vall", [128, 33, 64], BF16)
                cosT = sbuf(ph, "cosT", [128, 32, 8], F32)
                sinT = sbuf(ph, "sinT", [128, 32, 8], F32)
                hb = sbuf(ph, "hb", [128, 16], F32)
                idi = sbuf(ph, "idi256", [128, 256], I32)
                mv1 = sbuf(ph, "mv1", [128, 256], F32)
                mv2 = sbuf(ph, "mv2", [128, 256], F32)
                maskN = sbuf(ph, "maskN", [128, 256], F32)
                mask0 = sbuf(ph, "mask0", [128, 256], F32)
                qr = sbuf(ph, "qr", [128, 512], BF16)
                kr2 = sbuf(ph, "kr2", [128, 128], BF16)
                rt = [sbuf(ph, "rt%d" % i, [128, 8, 8], F32) for i in range(4)]
                rk = [sbuf(ph, "rk%d" % i, [128, 1, 8], F32) for i in range(4)]
                qT = sbuf(ph, "qT", [128, 4, 128], BF16)
                qTz = [sbuf(ph, "qTz%d" % i, [128, 4, 128], BF16) for i in range(2)]
                sm = [sbuf(ph, "sm%d" % i, [128, 2, 256], F32) for i in range(2)]
                pexp = [sbuf(ph, "pexp%d" % i, [128, 256], BF16) for i in range(4)]
                PT = [sbuf(ph, "PT%d" % i, [128, 2, 128], BF16) for i in range(4)]
                mx = sbuf(ph, "mx", [128, 8], F32)
                mneg = sbuf(ph, "mneg", [128, 8], F32)
                rs = sbuf(ph, "rs", [128, 8], F32)
                es_ = sbuf(ph, "es", [128, 8], F32)
                rinv = sbuf(ph, "rinv", [128, 8], F32)
                ao = sbuf(ph, "ao", [128, 512], BF16)
                aoT = [sbuf(ph, "aoT%d" % i, [128, 4, 512], BF16) for i in range(2)]
                pq = psum(ph, "pq", [128, 512])
                pkv = psum(ph, "pkv", [128, 512])
                ptr = psum(ph, "ptr", [128, 8, 128], BF16)
                ps = [psum(ph, "ps%d" % i, [128, 2, 256]) for i in range(2)]
                ppt = [psum(ph, "ppt%d" % i, [128, 8, 128], BF16) for i in range(2)]
                po = psum(ph, "po", [128, 8, 64])
                b_watt = [Buf(), Buf()]
                b_u2s = [[Buf(), Buf()] for _ in range(2)]
                b_kT2 = [Buf() for _ in range(33)]
                b_V = [Buf() for _ in range(33)]
                b_tab, b_hb, b_mask, b_idi, b_mv = Buf(), Buf(), Buf(), Buf(), Buf()
                b_qr, b_kr2, b_qT, b_ao = Buf(), Buf(), Buf(), Buf()
                b_qr_a, b_kr2_a, b_kr2_c = Buf(), Buf(), Buf()
                b_rt = [Buf() for _ in range(4)]
                b_rk = [Buf() for _ in range(4)]
                b_sm = [Buf(), Buf()]
                b_pexp = [Buf() for _ in range(4)]
                b_PT = [Buf() for _ in range(4)]
                b_mx, b_mneg, b_rs, b_es, b_rinv = [[Buf() for _ in range(4)] for _ in range(5)]
                b_aoT = [Buf(), Buf()]
                b_pq, b_pkv, b_ptr = Buf(excl=True), Buf(excl=True), Buf(excl=True)
                _bpo = Buf(excl=True)
                b_po = [_bpo] * 4
                b_ps = [Buf(excl=True), Buf(excl=True)]
                b_ppt = [Buf(excl=True), Buf(excl=True)]
                ds_w = [S.new_dsem() for _ in range(2)]
                ds_u = [[S.new_dsem() for _ in range(2)] for _ in range(2)]
                ds_m = S.new_dsem()
                ds_o = [S.new_dsem() for _ in range(2)]

                wv = watt_d[gi].rearrange("(kc p) j -> p kc j", p=128)
                import os as _os
                _skip = _os.environ.get("ATT_SKIP", "")
                if "w" not in _skip:
                  for h in range(2):
                    S.dma("pool", watt[:, 8 * h:8 * h + 8, :], wv[:, 8 * h:8 * h + 8, :], ds_w[h], writes=[b_watt[h]])
                if "h" not in _skip:
                    S.dma("sp", hb[:], hrow_d[gi].partition_broadcast(128), ds_m, writes=[b_hb])
                if "r" not in _skip:
                    build_rope_tables(ph, cosT, sinT, b_tab)
                if "m" in _skip:
                    S.barrier()
                    S.emit(block)
                    return
                S.op("pool", lambda e: e.iota(idi[:], pattern=[[1, 256]], base=0, channel_multiplier=-1), writes=[b_idi])
                S.ts("dve", mv1[:], idi[:], 1, ALU.is_ge, reads=[b_idi], writes=[b_mv])
                S.ts("dve", mv2[:], idi[:], 129, ALU.is_ge, reads=[b_idi], writes=[b_mv])
                S.tt("dve", mv1[:], mv1[:], mv2[:], ALU.subtract, reads=[b_mv], writes=[b_mv])
                S.ts("dve", maskN[:], mv1[:], -1.0, ALU.add, s2=30000.0, op1=ALU.mult, reads=[b_mv], writes=[b_mask])
                S.copy("dve", mask0[:], maskN[:], reads=[b_mask], writes=[b_mask])
                S.memset("dve", mask0[:, 0:128], -30000.0, writes=[b_mask])
                S.memset("dve", qTz[0][:], 0.0, writes=[b_qT])
                S.memset("dve", qTz[1][:], 0.0, writes=[b_qT])
                S.memset("dve", kT2[:, 0:128], 0.0, writes=[b_kT2[0]])
                S.memset("dve", Vall[:, 0, :], 0.0, writes=[b_V[0]])

                def rope(src3, dst3, nh, tmp, b_tmp, cos_i, sin_i, b_src, b_dst):
                    cb = cos_i.unsqueeze(1).to_broadcast([128, nh, 8])
                    sb_ = sin_i.unsqueeze(1).to_broadcast([128, nh, 8])
                    S.tt("dve", tmp[0][:], src3[:, :, 0:8], cb, ALU.mult, reads=[b_src, b_tab], writes=[b_tmp[0]])
                    S.tt("dve", tmp[1][:], src3[:, :, 8:16], sb_, ALU.mult, reads=[b_src, b_tab], writes=[b_tmp[1]])
                    S.tt("dve", tmp[2][:], src3[:, :, 8:16], cb, ALU.mult, reads=[b_src, b_tab], writes=[b_tmp[2]])
                    S.tt("dve", tmp[3][:], src3[:, :, 0:8], sb_, ALU.mult, reads=[b_src, b_tab], writes=[b_tmp[3]])
                    S.tt("dve", dst3[:, :, 0:8], tmp[0][:], tmp[1][:], ALU.subtract, reads=[b_tmp[0], b_tmp[1]], writes=[b_dst])
                    S.tt("dve", dst3[:, :, 8:16], tmp[2][:], tmp[3][:], ALU.add, reads=[b_tmp[2], b_tmp[3]], writes=[b_dst])

                _stop = int(_os.environ.get("ATT_STOP", "99"))
                for i in range(cfg.get("attn_tiles", 32)):
                    st, sub = i // 4, i % 4
                    sb2 = st % 2
                    if sub == 0:
                        for h in range(2):
                            S.dma("sp", u2s[sb2][:, 8 * h:8 * h + 8, :], u2_view(st)[:, 8 * h:8 * h + 8, :], ds_u[sb2][h],
                                  reads=[b_u2T[g] for g in range(NGRP)] if not cfg.get("mixer_test") else [],
                                  writes=[b_u2s[sb2][h]])
                    for kc in range(KC):
                        S.mm(pq[:], u2s[sb2][:, kc, sub * 128:(sub + 1) * 128], watt[:, kc, 0:512], start=(kc == 0), stop=(kc == KC - 1),
                             reads=[b_u2s[sb2][kc // 8], b_watt[kc // 8]], writes=[b_pq])
                    for kc in range(KC):
                        S.mm(pkv[:, 0:128], u2s[sb2][:, kc, sub * 128:(sub + 1) * 128], watt[:, kc, 512:640], start=(kc == 0), stop=(kc == KC - 1),
                             reads=[b_u2s[sb2][kc // 8], b_watt[kc // 8]], writes=[b_pkv])
                    if _stop == 1:
                        break
                    pq3 = pq[:].rearrange("p (h d) -> p h d", h=8)
                    qr3 = qr[:].rearrange("p (h d) -> p h d", h=8)
                    _sub = _os.environ.get("ATT_SUB", "adkv")
                    if "a" in _sub:
                        S.copy("dve", qr[:], pq[:], reads=[b_pq], writes=[b_qr])
                    if "d" in _sub:
                        rope(pq3, qr3, 8, rt, b_rt, cosT[:, i, :], sinT[:, i, :], b_pq, b_qr)
                    pk3 = pkv[:, 0:64].rearrange("p (h d) -> p h d", h=1)
                    kr3 = kr2[:, 0:64].rearrange("p (h d) -> p h d", h=1)
                    if "k" in _sub:
                        S.copy("dve", kr2[:, 0:64], pkv[:, 0:64], reads=[b_pkv], writes=[b_kr2])
                        rope(pk3, kr3, 1, rk, b_rk, cosT[:, i, :], sinT[:, i, :], b_pkv, b_kr2)
                        S.copy("dve", kr2[:, 64:128], kr2[:, 0:64], reads=[b_kr2, b_kr2_a], writes=[b_kr2_c])
                    if "v" in _sub:
                        S.copy("dve", Vall[:, i + 1, :], pkv[:, 64:128], reads=[b_pkv], writes=[b_V[i + 1]])
                    if _stop == 2:
                        break
                    for hp in range(4):
                        S.tr(ptr[:, hp, :], qr[:, hp * 128:(hp + 1) * 128], identB[:], reads=[b_qr, b_qr_a, b_const], writes=[b_ptr])
                    S.tr(ptr[:, 4, :], kr2[:], identB[:], reads=[b_kr2, b_kr2_a, b_kr2_c, b_const], writes=[b_ptr])
                    S.copy("dve", qTz[0][0:64, :, :], ptr[0:64, 0:4, :], reads=[b_ptr], writes=[b_qT])
                    S.copy("dve", qTz[1][64:128, :, :], ptr[64:128, 0:4, :], reads=[b_ptr], writes=[b_qT])
                    S.copy("dve", kT2[:, (i + 1) * 128:(i + 2) * 128], ptr[:, 4, :], reads=[b_ptr], writes=[b_kT2[i + 1]])
                    if _stop == 3:
                        break
                    mask = mask0 if i == 0 else maskN
                    for hp in range(4):
                        b2 = hp % 2
                        for e_ in range(2):
                            base = 64 * e_
                            S.mm(ps[b2][:, e_, :], qTz[e_][:, hp, :], kT2[:, i * 128:(i + 2) * 128],
                                 reads=[b_qT, b_kT2[i], b_kT2[i + 1]], writes=[b_ps[b2]])
                        S.tt("dve", sm[b2][:], ps[b2][:], mask[:].unsqueeze(1).to_broadcast([128, 2, 256]), ALU.add,
                             reads=[b_ps[b2], b_mask], writes=[b_sm[b2]])
                        if _stop == 40:
                            break
                        S.op("dve", (lambda b2=b2, hp=hp: (lambda e: e.tensor_reduce(out=mx[:, 2 * hp:2 * hp + 2], in_=sm[b2][:], axis=AX.X, op=ALU.max)))(),
                             reads=[b_sm[b2]], writes=[b_mx[hp]])
                        if _stop == 41:
                            break
                        S.stt("dve", mneg[:, 2 * hp:2 * hp + 2], mx[:, 2 * hp:2 * hp + 2], 0.125, hb[:, 8 + 2 * hp:10 + 2 * hp], ALU.mult, ALU.max,
                              reads=[b_mx[hp], b_hb], writes=[b_mneg[hp]])
                        S.ts("dve", mneg[:, 2 * hp:2 * hp + 2], mneg[:, 2 * hp:2 * hp + 2], -1.0, ALU.mult, reads=[b_mneg[hp]], writes=[b_mneg[hp]])
                        S.tt("dve", es_[:, 2 * hp:2 * hp + 2], hb[:, 8 + 2 * hp:10 + 2 * hp], mneg[:, 2 * hp:2 * hp + 2], ALU.add,
                             reads=[b_hb, b_mneg[hp]], writes=[b_es[hp]])
                        if _stop == 4:
                            break
                        S.memset("dve", rs[:, 2 * hp:2 * hp + 2], 0.0, writes=[b_rs[hp]])
                        for e_ in range(2):
                            h = 2 * hp + e_
                            pi = (2 * hp + e_) % 4
                            S.act(pexp[pi][:], sm[b2][:, e_, :], AF.Exp, bias=mneg[:, h:h + 1], scale=0.125, accum=rs[:, h:h + 1],
                                  reads=[b_sm[b2], b_mneg[hp]], writes=[b_pexp[pi], b_rs[hp]])
                        S.act(es_[:, 2 * hp:2 * hp + 2], es_[:, 2 * hp:2 * hp + 2], AF.Exp, reads=[b_es[hp]], writes=[b_es[hp]])
                        S.tt("dve", rinv[:, 2 * hp:2 * hp + 2], rs[:, 2 * hp:2 * hp + 2], es_[:, 2 * hp:2 * hp + 2], ALU.add,
                             reads=[b_rs[hp], b_es[hp]], writes=[b_rinv[hp]])
                        S.op("dve", (lambda hp=hp: (lambda e: e.reciprocal(out=rinv[:, 2 * hp:2 * hp + 2], in_=rinv[:, 2 * hp:2 * hp + 2])))(),
                             reads=[b_rinv[hp]], writes=[b_rinv[hp]])
                        if _stop == 5:
                            break
                        for e_ in range(2):
                            h = 2 * hp + e_
                            pi = (2 * hp + e_) % 4
                            for kt in range(2):
                                S.tr(ppt[e_][:, kt, :], pexp[pi][:, kt * 128:(kt + 1) * 128], identB[:],
                                     reads=[b_pexp[pi], b_const], writes=[b_ppt[e_]])
                            S.copy("act" if e_ else "dve", PT[pi][:], ppt[e_][:, 0:2, :], reads=[b_ppt[e_]], writes=[b_PT[pi]])
                            for kt in range(2):
                                S.mm(po[:, h, :], PT[pi][:, kt, :], Vall[:, i + kt, :], start=(kt == 0), stop=(kt == 1),
                                     reads=[b_PT[pi], b_V[i + kt]], writes=[b_po[hp]])
                            S.act(ao[:, h * 64:(h + 1) * 64], po[:, h, :], AF.Identity, scale=rinv[:, h:h + 1],
                                  reads=[b_po[hp], b_rinv[hp]], writes=[b_ao])
                    if _stop in (4, 40, 41, 5, 6):
                        break
                    for hp in range(4):
                        S.tr(ptr[:, hp, :], ao[:, hp * 128:(hp + 1) * 128], identB[:], reads=[b_ao, b_const], writes=[b_ptr])
                    S.copy("dve", aoT[sb2][:, :, sub * 128:(sub + 1) * 128], ptr[:, 0:4, :], reads=[b_ptr], writes=[b_aoT[sb2]])
                    if sub == 3:
                        S.dma("sp", mixT_d[gi * 8:gi * 8 + 4, :, st * 512:(st + 1) * 512].rearrange("k p t -> p k t"), aoT[sb2][:],
                              ds_o[sb2], reads=[b_aoT[sb2]], writes=[b_mixT_parts[gi][0]])
                S.barrier()
                S.emit(block)

        def dn_pass(gi):
            with ExitStack() as ph, nc.Block() as block:
                H = 4
                wdn = sbuf(ph, "wdn", [128, KC, 1536], BF16)
                wgt = sbuf(ph, "wgt", [128, KC, 512], BF16)
                wab = sbuf(ph, "wab", [128, KC, 8], BF16)
                u2s = [sbuf(ph, "u2d%d" % i, [128, KC, 512], BF16) for i in range(2)]
                cw = sbuf(ph, "cw", [128, 12, 4], F32)
                hb = sbuf(ph, "hbd", [128, 16], F32)
                nwb = sbuf(ph, "nwb", [128, 128], F32)
                nea = sbuf(ph, "nea", [128, 4], F32)
                zb = [sbuf(ph, "zb%d" % i, [128, 515], F32) for i in range(2)]
                halo = sbuf(ph, "halo", [128, 12, 3], F32)
                cacc = [sbuf(ph, "cacc%d" % i, [128, 512], F32) for i in range(2)]
                ysil = [sbuf(ph, "ysil%d" % i, [128, 512], F32) for i in range(2)]
                sqb = [sbuf(ph, "sqb%d" % i, [128, 512], BF16) for i in range(2)]
                rnb = [sbuf(ph, "rnb%d" % i, [128, 512], F32) for i in range(2)]
                QT = sbuf(ph, "QT", [128, H, 512], BF16)
                KT = sbuf(ph, "KT", [128, H, 512], BF16)
                VT = sbuf(ph, "VT", [128, H, 512], F32)
                idi = sbuf(ph, "idid", [128, 128], I32)
                Tri2 = sbuf(ph, "Tri2", [128, 128], F32)
                MMs = sbuf(ph, "MMs", [128, 128], F32)
                onesF = sbuf(ph, "onesF", [128, 128], F32)
                ab = sbuf(ph, "ab", [128, 4, 8], F32)
                xp = sbuf(ph, "xp", [128, 4, 4], F32)
                xn = sbuf(ph, "xn", [128, 4, 4], F32)
                gstep = sbuf(ph, "gstep", [128, 4, 4], F32)
                beta = sbuf(ph, "beta", [128, 4, 4], F32)
                nbeta = sbuf(ph, "nbeta", [128, 4, 4], F32)
                gcum = sbuf(ph, "gcum", [128, 4, 4], F32)
                egc = sbuf(ph, "egc", [128, 4, 4], F32)
                bge = sbuf(ph, "bge", [128, 4, 4], F32)
                gld = sbuf(ph, "gld", [128, 4, 4], F32)
                gws = [sbuf(ph, "gws%d" % i, [128, 512], F32) for i in range(2)]
                gb = [sbuf(ph, "gb%d" % i, [128, 128], F32) for i in range(2)]
                EG = [sbuf(ph, "EG%d" % i, [128, 128], F32) for i in range(2)]
                dls = [sbuf(ph, "dls%d" % i, [128, 128], F32) for i in range(2)]
                dl = [sbuf(ph, "dl%d" % i, [128, 128], F32) for i in range(2)]
                Xs = [sbuf(ph, "Xs%d" % i, [128, 128], F32) for i in range(2)]
                Ys = [sbuf(ph, "Ys%d" % i, [128, 128], F32) for i in range(2)]
                TTs = [sbuf(ph, "TTs%d" % i, [128, 128], F32) for i in range(2)]
                TTb = sbuf(ph, "TTb", [128, 128], BF16)
                a_sb = sbuf(ph, "a_sb", [128, 128], BF16)
                aT = sbuf(ph, "aT", [128, 128], BF16)
                Vb = sbuf(ph, "Vb", [128, 128], BF16)
                Kbg = sbuf(ph, "Kbg", [128, 128], BF16)
                kdec = sbuf(ph, "kdec", [128, 128], BF16)
                u_sb = sbuf(ph, "u_sb", [128, 128], F32)
                wT = sbuf(ph, "wT", [128, 128], BF16)
                QgT = sbuf(ph, "QgT", [128, 128], BF16)
                vnw = [sbuf(ph, "vnw%d" % i, [128, 128], BF16) for i in range(2)]
                S32 = sbuf(ph, "S32", [128, H, 128], F32)
                Sb = sbuf(ph, "Sb", [128, H, 128], BF16)
                osq = sbuf(ph, "osq", [128, 128], F32)
                oss = sbuf(ph, "oss", [128, 1], F32)
                on = [sbuf(ph, "on%d" % i, [128, 512], BF16) for i in range(2)]
                onT = [sbuf(ph, "onT%d" % i, [128, H, 512], BF16) for i in range(2)]
                pz = [psum(ph, "pz%d" % i, [128, 512]) for i in range(2)]
                pss = psum(ph, "pss", [128, 512])
                pab = psum(ph, "pab", [128, 512])
                pGf = psum(ph, "pG", [128, 512])
                pG = pGf[:, 0:128]
                pn = psum(ph, "pn", [128, 4, 128])
                pnb = psum(ph, "pnb", [128, 8, 128], BF16)
                psc = psum(ph, "psc", [128, 4, 128])
                B = {}

                def bf(name):
                    if name not in B:
                        B[name] = Buf(name, excl=name in ("pz0", "pz1", "pss", "pab", "pG", "pn", "pnb", "psc"))
                    return B[name]
                ds_w = [S.new_dsem() for _ in range(7)]
                ds_u = [[S.new_dsem() for _ in range(2)] for _ in range(2)]
                ds_m = [S.new_dsem() for _ in range(3)]
                ds_o = [S.new_dsem() for _ in range(2)]

                wv = wdn_d[gi].rearrange("(kc p) j -> p kc j", p=128)
                for q in range(4):
                    S.dma("pool", wdn[:, 4 * q:4 * q + 4, :], wv[:, 4 * q:4 * q + 4, :], ds_w[q], writes=[bf("wdn%d" % q)])
                gv = wgate_d[gi].rearrange("(kc p) j -> p kc j", p=128)
                for h in range(2):
                    S.dma("pool", wgt[:, 8 * h:8 * h + 8, :], gv[:, 8 * h:8 * h + 8, :], ds_w[4 + h], writes=[bf("wgt%d" % h)])
                S.dma("pool", wab[:], wab_d[gi].rearrange("(kc p) j -> p kc j", p=128), ds_w[6], writes=[bf("wab")])
                S.dma("sp", cw[:], convw_d[gi], ds_m[0], writes=[bf("cw")])
                S.dma("sp", hb[:], hrow_d[gi].partition_broadcast(128), ds_m[1], writes=[bf("hb")])
                S.dma("sp", nwb[:], dnw_d.partition_broadcast(128), ds_m[2], writes=[bf("nwb")])
                S.act(nea[:], hb[:, 0:4], AF.Exp, reads=[bf("hb")], writes=[bf("nea")])
                S.ts("dve", nea[:], nea[:], -1.0, ALU.mult, reads=[bf("nea")], writes=[bf("nea")])
                S.op("pool", lambda e: e.iota(idi[:], pattern=[[1, 128]], base=0, channel_multiplier=-1), writes=[bf("idi")])
                S.ts("dve", Tri2[:], idi[:], 0, ALU.is_ge, reads=[bf("idi")], writes=[bf("masks")])
                S.memset("dve", Tri2[0:64, 64:128], 0.0, writes=[bf("masks")])
                S.ts("dve", MMs[:], idi[:], 0, ALU.is_ge, s2=-200.0, op1=ALU.mult, reads=[bf("idi")], writes=[bf("masks")])
                S.memset("dve", MMs[64:128, 0:64], -200.0, writes=[bf("masks")])
                S.memset("dve", onesF[:], 1.0, writes=[bf("masks")])
                S.memset("dve", halo[:], 0.0, writes=[bf("halo")])
                S.memset("dve", vnw[0][:], 0.0, writes=[bf("vnew0")])
                S.memset("dve", vnw[1][:], 0.0, writes=[bf("vnew1")])
                S.memset("dve", S32[:], 0.0, writes=[bf("S32_%d" % h) for h in range(H)])
                S.memset("dve", Sb[:], 0.0, writes=[bf("Sb_%d" % h) for h in range(H)])
                cnt = {"z": 0, "n": 0, "nb": 0, "sc": 0, "pp": 0}

                def pn_slot():
                    cnt["n"] += 1
                    j = cnt["n"] % 4
                    return pn[:, j, :], bf("pn")

                def pnb_slot():
                    cnt["nb"] += 1
                    j = cnt["nb"] % 4
                    return pnb[:, j, :], bf("pnb")

                def psc_slot():
                    cnt["sc"] += 1
                    j = cnt["sc"] % 4
                    return psc[:, j, :], bf("psc")

                for st in range(cfg.get("dn_st", 8)):
                    sb2 = st % 2
                    for h in range(2):
                        S.dma("sp", u2s[sb2][:, 8 * h:8 * h + 8, :], u2_view(st)[:, 8 * h:8 * h + 8, :], ds_u[sb2][h],
                              writes=[bf("u2s%d_%d" % (sb2, h))])
                    u2r = [bf("u2s%d_0" % sb2), bf("u2s%d_1" % sb2)]
                    for c in range(12):
                        typ, hh = c // 4, c % 4
                        zi = cnt["z"] % 2
                        cnt["z"] += 1
                        for kc in range(KC):
                            S.mm(pz[zi][:], wdn[:, kc, c * 128:(c + 1) * 128], u2s[sb2][:, kc, :], start=(kc == 0), stop=(kc == KC - 1),
                                 reads=[bf("wdn%d" % (kc // 4)), u2r[kc // 8]], writes=[bf("pz%d" % zi)])
                        S.copy("act", zb[zi][:, 3:515], pz[zi][:], reads=[bf("pz%d" % zi)], writes=[bf("zb%d" % zi)])
                        S.copy("dve", zb[zi][:, 0:3], halo[:, c, :], reads=[bf("halo")], writes=[bf("zbh%d" % zi)])
                        zr = [bf("zb%d" % zi), bf("zbh%d" % zi)]
                        S.ts("dve", cacc[zi][:], zb[zi][:, 3:515], cw[:, c, 3:4], ALU.mult, reads=zr + [bf("cw")], writes=[bf("cacc%d" % zi)])
                        for j in range(3):
                            S.stt("dve", cacc[zi][:], zb[zi][:, j:j + 512], cw[:, c, j:j + 1], cacc[zi][:], ALU.mult, ALU.add,
                                  reads=zr + [bf("cw"), bf("cacc%d" % zi)], writes=[bf("cacc%d" % zi)])
                        S.copy("dve", halo[:, c, :], zb[zi][:, 512:515], reads=[bf("zb%d" % zi)], writes=[bf("halo")])
                        if typ == 2:
                            S.act(VT[:, hh, :], cacc[zi][:], AF.Silu, reads=[bf("cacc%d" % zi)], writes=[bf("VT%d" % hh)])
                        else:
                            S.act(ysil[zi][:], cacc[zi][:], AF.Silu, reads=[bf("cacc%d" % zi)], writes=[bf("ysil%d" % zi)])
                            S.act(sqb[zi][:], ysil[zi][:], AF.Square, reads=[bf("ysil%d" % zi)], writes=[bf("sqb%d" % zi)])
                            S.mm(pss[:], onesB[:], sqb[zi][:], reads=[bf("sqb%d" % zi), b_const], writes=[bf("pss")])
                            S.act(rnb[zi][:], pss[:], AF.Sqrt, bias=EPS, scale=1.0, reads=[bf("pss")], writes=[bf("rnb%d" % zi)])
                            S.op("dve", (lambda zi=zi: (lambda e: e.reciprocal(out=rnb[zi][:], in_=rnb[zi][:])))(),
                                 reads=[bf("rnb%d" % zi)], writes=[bf("rnb%d" % zi)])
                            if typ == 0:
                                S.stt("dve", QT[:, hh, :], ysil[zi][:], float(128 ** -0.5), rnb[zi][:], ALU.mult, ALU.mult,
                                      reads=[bf("ysil%d" % zi), bf("rnb%d" % zi)], writes=[bf("QT%d" % hh)])
                            else:
                                S.tt("dve", KT[:, hh, :], ysil[zi][:], rnb[zi][:], ALU.mult,
                                     reads=[bf("ysil%d" % zi), bf("rnb%d" % zi)], writes=[bf("KT%d" % hh)])
                    for sub in range(4):
                        for kc in range(KC):
                            S.mm(pab[:, sub * 8:(sub + 1) * 8], u2s[sb2][:, kc, sub * 128:(sub + 1) * 128], wab[:, kc, :],
                                 start=(kc == 0), stop=(kc == KC - 1), reads=[u2r[kc // 8], bf("wab")], writes=[bf("pab")])
                    S.copy("dve", ab[:], pab[:, 0:32].rearrange("p (s j) -> p s j", s=4), reads=[bf("pab")], writes=[bf("ab")])
                    dtb = hb[:, 4:8].unsqueeze(1).to_broadcast([128, 4, 4])
                    S.tt("dve", xp[:], ab[:, :, 0:4], dtb, ALU.add, reads=[bf("ab"), bf("hb")], writes=[bf("xp")])
                    S.ts("dve", xn[:], xp[:], 0.0, ALU.min, reads=[bf("xp")], writes=[bf("xn")])
                    S.ts("dve", xp[:], xp[:], 0.0, ALU.max, reads=[bf("xp")], writes=[bf("xp")])
                    S.tt("dve", xn[:], xn[:], xp[:], ALU.subtract, reads=[bf("xn"), bf("xp")], writes=[bf("xn")])
                    S.act(xn[:], xn[:], AF.Exp, reads=[bf("xn")], writes=[bf("xn")])
                    S.act(xn[:], xn[:], AF.Ln, bias=1.0, reads=[bf("xn")], writes=[bf("xn")])
                    S.tt("dve", xp[:], xp[:], xn[:], ALU.add, reads=[bf("xn"), bf("xp")], writes=[bf("xp")])
                    S.tt("dve", gstep[:], xp[:], nea[:].unsqueeze(1).to_broadcast([128, 4, 4]), ALU.mult,
                         reads=[bf("xp"), bf("nea")], writes=[bf("gstep")])
                    S.act(beta[:], ab[:, :, 4:8], AF.Exp, scale=-1.0, reads=[bf("ab")], writes=[bf("beta")])
                    S.ts("dve", beta[:], beta[:], 1.0, ALU.add, reads=[bf("beta")], writes=[bf("beta")])
                    S.op("dve", lambda e: e.reciprocal(out=beta[:], in_=beta[:]), reads=[bf("beta")], writes=[bf("beta")])
                    S.ts("dve", nbeta[:], beta[:], -1.0, ALU.mult, reads=[bf("beta")], writes=[bf("nbeta")])
                    S.mm(pab[:, 32:48], Tri2[:], gstep[:].rearrange("p s h -> p (s h)"), reads=[bf("masks"), bf("gstep")], writes=[bf("pab")])
                    S.copy("dve", gcum[:], pab[:, 32:48].rearrange("p (s h) -> p s h", s=4), reads=[bf("pab")], writes=[bf("gcum")])
                    S.act(egc[:], gcum[:], AF.Exp, reads=[bf("gcum")], writes=[bf("egc")])
                    S.tt("dve", bge[:], egc[:], beta[:], ALU.mult, reads=[bf("egc"), bf("beta")], writes=[bf("bge")])
                    for sub in range(4):
                        gi2 = sub % 2
                        for kc in range(KC):
                            S.mm(pss[:], u2s[sb2][:, kc, sub * 128:(sub + 1) * 128], wgt[:, kc, :], start=(kc == 0), stop=(kc == KC - 1),
                                 reads=[u2r[kc // 8], bf("wgt%d" % (kc // 8))], writes=[bf("pss")])
                        S.act(gws[gi2][:], pss[:], AF.Silu, reads=[bf("pss")], writes=[bf("gws%d" % gi2)])
                        S.tt("dve", gws[gi2][:].rearrange("p (h v) -> p h v", h=H), gws[gi2][:].rearrange("p (h v) -> p h v", h=H),
                             nwb[:].unsqueeze(1).to_broadcast([128, H, 128]), ALU.mult,
                             reads=[bf("gws%d" % gi2), bf("nwb")], writes=[bf("gws%d" % gi2)])
                        tsl = slice(sub * 128, (sub + 1) * 128)
                        for hh in range(H):
                            x2 = (sub * H + hh) % 2
                            gcol = gcum[:, sub, hh:hh + 1]
                            S.ts("dve", gb[x2][:], onesF[:], gstep[:, sub, hh:hh + 1], ALU.mult, reads=[bf("masks"), bf("gstep")], writes=[bf("gb%d" % x2)])
                            S.mm(pG, gb[x2][:], Tri2[:], reads=[bf("gb%d" % x2), bf("masks")], writes=[bf("pG")])
                            S.act(EG[x2][:], pG, AF.Exp, reads=[bf("pG")], writes=[bf("EG%d" % x2)])
                            S.ts("dve", dls[x2][:], pG, -1.0, ALU.mult, s2=gcol, op1=ALU.add, reads=[bf("pG"), bf("gcum")], writes=[bf("dls%d" % x2)])
                            S.tt("dve", dls[x2][:], dls[x2][:], MMs[:], ALU.min, reads=[bf("dls%d" % x2), bf("masks")], writes=[bf("dls%d" % x2)])
                            S.act(dls[x2][:], dls[x2][:], AF.Exp, reads=[bf("dls%d" % x2)], writes=[bf("dls%d" % x2)])
                            S.tt("dve", dl[x2][:], dls[x2][:], identF[:], ALU.add, reads=[bf("dls%d" % x2), b_const], writes=[bf("dl%d" % x2)])
                            for half in range(2):
                                r = slice(64 * half, 64 * half + 64)
                                lc = 64 * half + 63
                                S.tt("dve", gld[r, sub, hh:hh + 1], pG[r, lc:lc + 1], gcum[r, sub, hh:hh + 1], ALU.subtract,
                                     reads=[bf("gcum"), bf("pG")], writes=[bf("gld")])
                            S.act(gld[:, sub, hh:hh + 1], gld[:, sub, hh:hh + 1], AF.Exp, reads=[bf("gld")], writes=[bf("gld")])
                            pgr, bgr = pn_slot()
                            S.mm(pgr, KT[:, hh, tsl], KT[:, hh, tsl], reads=[bf("KT%d" % hh)], writes=[bgr])
                            pqk, bqk = pn_slot()
                            S.mm(pqk, QT[:, hh, tsl], KT[:, hh, tsl], reads=[bf("QT%d" % hh), bf("KT%d" % hh)], writes=[bqk])
                            S.stt("dve", Xs[0][:], pgr, nbeta[:, sub, hh:hh + 1], dls[x2][:], ALU.mult, ALU.mult,
                                  reads=[bgr, bf("nbeta"), bf("dls%d" % x2)], writes=[bf("X0")])
                            S.tt("dve", a_sb[:], pqk, dl[x2][:], ALU.mult, reads=[bqk, bf("dl%d" % x2)], writes=[bf("a_sb")])
                            pat, bat = pnb_slot()
                            S.tr(pat, a_sb[:], identB[:], reads=[bf("a_sb"), b_const], writes=[bat])
                            S.copy("act", aT[:], pat, reads=[bat], writes=[bf("aT")])
                            py, by = pn_slot()
                            S.tr(py, Xs[0][:], identF[:], reads=[bf("X0"), b_const], writes=[by])
                            S.copy("act", Ys[0][:], py, reads=[by], writes=[bf("Y0")])
                            S.tt("dve", TTs[0][:], py, identF[:], ALU.add, reads=[by, b_const], writes=[bf("TT0")])
                            cur = 0
                            for lvl in range(5):
                                nxt = 1 - cur
                                px, bx = pn_slot()
                                S.mm(px, Ys[cur][:], Xs[cur][:], reads=[bf("Y%d" % cur), bf("X%d" % cur)], writes=[bx])
                                S.copy("act", Xs[nxt][:], px, reads=[bx], writes=[bf("X%d" % nxt)])
                                if lvl < 4:
                                    py2, by2 = pn_slot()
                                    S.mm(py2, Xs[cur][:], Ys[cur][:], reads=[bf("Y%d" % cur), bf("X%d" % cur)], writes=[by2])
                                    S.copy("dve", Ys[nxt][:], py2, reads=[by2], writes=[bf("Y%d" % nxt)])
                                pt_, bt_ = pn_slot()
                                S.mm(pt_, Xs[nxt][:], TTs[cur][:], reads=[bf("X%d" % nxt), bf("TT%d" % cur)], writes=[bt_])
                                if lvl < 4:
                                    S.tt("dve", TTs[nxt][:], pt_, TTs[cur][:], ALU.add, reads=[bt_, bf("TT%d" % cur)], writes=[bf("TT%d" % nxt)])
                                else:
                                    S.tt("dve", TTb[:], pt_, TTs[cur][:], ALU.add, reads=[bt_, bf("TT%d" % cur)], writes=[bf("TTb")])
                                cur = nxt
                            pk_, bk_ = pnb_slot()
                            S.tr(pk_, KT[:, hh, tsl], identB[:], reads=[bf("KT%d" % hh), b_const], writes=[bk_])
                            S.ts("dve", Kbg[:], pk_, bge[:, sub, hh:hh + 1], ALU.mult, reads=[bk_, bf("bge")], writes=[bf("Kbg")])
                            S.act(kdec[:], pk_, AF.Identity, scale=gld[:, sub, hh:hh + 1], reads=[bk_, bf("gld")], writes=[bf("kdec")])
                            pv_, bv_ = pn_slot()
                            S.tr(pv_, VT[:, hh, tsl], identF[:], reads=[bf("VT%d" % hh), b_const], writes=[bv_])
                            S.act(Vb[:], pv_, AF.Identity, scale=beta[:, sub, hh:hh + 1], reads=[bv_, bf("beta")], writes=[bf("Vb")])
                            pu_, bu_ = pn_slot()
                            S.mm(pu_, TTb[:], Vb[:], reads=[bf("TTb"), bf("Vb")], writes=[bu_])
                            S.copy("act", u_sb[:], pu_, reads=[bu_], writes=[bf("u_sb")])
                            pw_, bw_ = pn_slot()
                            S.mm(pw_, Kbg[:], TTb[:], reads=[bf("TTb"), bf("Kbg")], writes=[bw_])
                            S.copy("dve", wT[:], pw_, reads=[bw_], writes=[bf("wT")])
                            S.tt("dve", QgT[:], QT[:, hh, tsl], EG[x2][:], ALU.mult, reads=[bf("QT%d" % hh), bf("EG%d" % x2)], writes=[bf("QgT")])
                            for half in range(2):
                                r = slice(64 * half, 64 * half + 64)
                                lc = 64 * half + 63
                                pws, bws = psc_slot()
                                vn = vnw[half]
                                bvn = bf("vnew%d" % half)
                                S.mm(pws, wT[:], Sb[:, hh, :], reads=[bf("wT"), bf("Sb_%d" % hh)], writes=[bws])
                                S.tt("dve", vn[r, :], u_sb[r, :], pws[r, :], ALU.subtract, reads=[bf("u_sb"), bws], writes=[bvn])
                                po_, bo_ = psc_slot()
                                S.mm(po_, QgT[:], Sb[:, hh, :], start=True, stop=False, reads=[bf("QgT"), bf("Sb_%d" % hh)], writes=[bo_])
                                S.mm(po_, aT[:], vn[:], start=False, stop=True, reads=[bf("aT"), bvn], writes=[bo_])
                                pds, bds = psc_slot()
                                S.mm(pds, kdec[:], vn[:], reads=[bf("kdec"), bvn], writes=[bds])
                                S.stt("dve", S32[:, hh, :], S32[:, hh, :], EG[x2][:, lc:lc + 1], pds, ALU.mult, ALU.add,
                                      reads=[bf("S32_%d" % hh), bf("EG%d" % x2), bds], writes=[bf("S32_%d" % hh)])
                                S.copy("act", Sb[:, hh, :], S32[:, hh, :], reads=[bf("S32_%d" % hh)], writes=[bf("Sb_%d" % hh)])
                                S.memset("dve", oss[r, :], 0.0, writes=[bf("oss")])
                                S.act(osq[r, :], po_[r, :], AF.Square, accum=oss[r, :], reads=[bo_, bf("oss")], writes=[bf("osq"), bf("oss")])
                                S.act(oss[r, :], oss[r, :], AF.Sqrt, bias=EPS, scale=1.0 / 128, reads=[bf("oss")], writes=[bf("oss")])
                                S.op("dve", (lambda r=r: (lambda e: e.reciprocal(out=oss[r, :], in_=oss[r, :])))(), reads=[bf("oss")], writes=[bf("oss")])
                                S.stt("dve", on[gi2][r, hh * 128:(hh + 1) * 128], po_[r, :], oss[r, :], gws[gi2][r, hh * 128:(hh + 1) * 128],
                                      ALU.mult, ALU.mult, reads=[bo_, bf("oss"), bf("gws%d" % gi2)], writes=[bf("on%d" % gi2)])
                        for hh in range(H):
                            pt2, bt2 = pnb_slot()
                            S.tr(pt2, on[gi2][:, hh * 128:(hh + 1) * 128], identB[:], reads=[bf("on%d" % gi2), b_const], writes=[bt2])
                            S.copy("act", onT[sb2][:, hh, tsl], pt2, reads=[bt2], writes=[bf("onT%d" % sb2)])
                    S.dma("sp", mixT_d[gi * 8 + 4:gi * 8 + 8, :, st * 512:(st + 1) * 512].rearrange("k p t -> p k t"), onT[sb2][:],
                          ds_o[sb2], reads=[bf("onT%d" % sb2)], writes=[b_mixT_parts[gi][1]])
                S.barrier()
                S.emit(block)

        b_mixT_parts = [[Buf(), Buf()] for _ in range(NG)]
        if not skip_mixer:
            for gi in range(NG):
                attn_pass(gi)
                if not cfg.get("skip_dn"):
                    dn_pass(gi)

        with ExitStack() as ph, nc.Block() as block:
            T = alloc_ffn(ph)
            woutv = wout_d.rearrange("(mc p) o -> p mc o", p=128)
            for g in range(NGRP):
                t0, t1 = g * TG, (g + 1) * TG
                S.dmas("sp", [(T["acc"][:, 4 * q:4 * q + 4, :], h1T_d[4 * q:4 * q + 4, :, t0:t1].rearrange("k p t -> p k t")) for q in range(4)],
                       T["ds_ld"][0], reads=[b_h1T[g]], writes=[T["b_acc"][kc][tt] for kc in range(KC) for tt in range(NTT)])
                if not skip_mixer:
                    S.dmas("sp", [(T["uT"][:, 4 * q:4 * q + 4, :],
                                   mixT_d[4 * q:4 * q + 4, :, TOK0 + t0:TOK0 + t1].rearrange("k p t -> p k t")) for q in range(4)],
                           T["ds_ld"][1], reads=[b_mixT], writes=[T["b_uT"][tt] for tt in range(NTT)])
                    for ob in range(D // FB):
                        s = ob % 2
                        for h in range(2):
                            S.dma("pool", T["wgb"][s][:, 8 * h:8 * h + 8, :], woutv[:, 8 * h:8 * h + 8, ob * FB:(ob + 1) * FB],
                                  T["ds_wg"][s][h], writes=[T["b_wg"][s][h]])
                        for oc2 in range(FB // 128):
                            oc = ob * (FB // 128) + oc2
                            for tt in range(NTT):
                                i = T["cnt"]["pd"] % 2
                                T["cnt"]["pd"] += 1
                                for mc in range(KC):
                                    S.mm(T["pd"][i][:], T["wgb"][s][:, mc, oc2 * 128:(oc2 + 1) * 128],
                                         T["uT"][:, mc, tt * 512:(tt + 1) * 512], start=(mc == 0), stop=(mc == KC - 1),
                                         reads=[T["b_wg"][s][mc // 8], T["b_uT"][tt]], writes=[T["b_pd"][i]])
                                a = T["acc"][:, oc, tt * 512:(tt + 1) * 512]
                                S.stt("dve", a, T["pd"][i][:], vec[:, G2, oc:oc + 1], a, ALU.mult, ALU.add,
                                      reads=[T["b_pd"][i], T["b_acc"][oc][tt], b_vec], writes=[T["b_acc"][oc][tt]])
                norm_group(T, GW3, SH3)
                ffn_group(T, wg_d[1], wu_d[1], wd_d[1], G3H)
                for tt in range(NTT):
                    rms_stats(T, tt)
                    for kc in range(KC):
                        a = T["acc"][:, kc, tt * 512:(tt + 1) * 512]
                        S.stt("dve", a, a, vec[:, FW, kc:kc + 1], T["rstd"][:], ALU.mult, ALU.mult,
                              reads=[T["b_acc"][kc][tt], T["b_rstd"], b_vec], writes=[T["b_acc"][kc][tt]])
                    for sub in range(4):
                        i = T["cnt"]["xin"] % 2
                        T["cnt"]["xin"] += 1
                        c0 = tt * 512 + sub * 128
                        for q in range(4):
                            for j in range(4):
                                kc = 4 * q + j
                                S.tr(T["pt"][:, j, :], T["acc"][:, kc, c0:c0 + 128], identF[:],
                                     reads=[T["b_acc"][kc][tt], b_const], writes=[T["b_pt"]])
                            S.copy("act" if q % 2 else "dve", T["xin"][i][:, q * 512:(q + 1) * 512],
                                   T["pt"][:].rearrange("p j d -> p (j d)"),
                                   reads=[T["b_pt"]], writes=[T["b_xin"][i]])
                        r0 = t0 + c0
                        S.dma("sp", out_d[r0:r0 + 128, :], T["xin"][i][:], T["ds_io"][i], reads=[T["b_xin"][i]])
            S.barrier()
            S.emit(block)
    return nc


def col_layout(v):
    v = np.asarray(v)
    return np.ascontiguousarray(v.reshape(-1, 128).T)


def make_in_map(inp, b, tok_lo, tok_hi, groups=(0, 1)):
    m = {}
    m["x"] = np.ascontiguousarray(inp["x"][b, tok_lo:tok_hi])
    m["ccol"] = col_layout(inp["c"][b])
    m["ada_w"] = np.ascontiguousarray(inp["ada_w"][0])
    m["adab_col"] = col_layout(inp["ada_b"][0])
    m["ncol"] = np.ascontiguousarray(np.stack([col_layout(inp["norm_ffn1"][0]), col_layout(inp["norm_mix"][0]),
                                               col_layout(inp["norm_ffn2"][0]), col_layout(inp["final_norm"])], axis=1))
    m["wg1"] = np.ascontiguousarray(inp["ffn1_w_gate"][0])
    m["wu1"] = np.ascontiguousarray(inp["ffn1_w_up"][0])
    m["wd1"] = np.ascontiguousarray(inp["ffn1_w_down"][0])
    m["wg2"] = np.ascontiguousarray(inp["ffn2_w_gate"][0])
    m["wu2"] = np.ascontiguousarray(inp["ffn2_w_up"][0])
    m["wd2"] = np.ascontiguousarray(inp["ffn2_w_down"][0])
    wout = inp["w_out"][0]
    w_in = inp["w_in"][0]
    conv_w = inp["conv_w"][0]
    rows = []
    for j, G in enumerate(groups):
        rows += [wout[512 * G:512 * G + 512], wout[1024 + 512 * G:1024 + 512 * G + 512]]
        m["watt%d" % j] = np.ascontiguousarray(np.concatenate(
            [w_in[:, 512 * G:512 * G + 512], w_in[:, 1024 + 64 * G:1024 + 64 * G + 64], w_in[:, 1152 + 64 * G:1152 + 64 * G + 64]], axis=1))
        m["wdn%d" % j] = np.ascontiguousarray(np.concatenate(
            [w_in[:, 1280 + 512 * G:1280 + 512 * G + 512], w_in[:, 2304 + 512 * G:2304 + 512 * G + 512],
             w_in[:, 3328 + 512 * G:3328 + 512 * G + 512]], axis=1))
        m["wgate%d" % j] = np.ascontiguousarray(w_in[:, 4352 + 512 * G:4352 + 512 * G + 512])
        m["wab%d" % j] = np.ascontiguousarray(np.concatenate(
            [w_in[:, 5376 + 4 * G:5376 + 4 * G + 4], w_in[:, 5384 + 4 * G:5384 + 4 * G + 4]], axis=1))
        cw = np.zeros((128, 12, 4), np.float32)
        for c in range(12):
            typ, hh = c // 4, 4 * G + c % 4
            ch0 = typ * 1024 + hh * 128
            cw[:, c, :] = conv_w[:, ch0:ch0 + 128].T
        m["convw%d" % j] = cw
        m["hrow%d" % j] = np.ascontiguousarray(np.concatenate(
            [inp["a_log"][0, 4 * G:4 * G + 4], inp["dt_bias"][0, 4 * G:4 * G + 4], inp["attn_sinks"][0, 8 * G:8 * G + 8]])[None, :]).astype(np.float32)
    other = [G for G in (0, 1) if G not in groups]
    for G in other:
        rows += [wout[512 * G:512 * G + 512], wout[1024 + 512 * G:1024 + 512 * G + 512]]
    m["wout"] = np.ascontiguousarray(np.concatenate(rows, axis=0))
    m["dnw"] = np.ascontiguousarray(inp["dn_norm_w"][0][None, :])
    m["pos"] = np.ascontiguousarray(inp["positions"][b].reshape(32, 128).T.astype(np.int32))
    return m


CFG = {"NT": 4096, "TG": 1024, "NG": 2, "TOK0": 0}


def kernel(**inputs):
    inp = {k: np.asarray(v) for k, v in inputs.items()}
    nc = build_program(CFG)
    in_maps = [make_in_map(inp, i, 0, 4096) for i in range(4)]
    res = run_bass_kernel_spmd(nc, in_maps, core_ids=[0, 2, 4, 6])
    out = np.stack([np.asarray(res.results[b]["out"]) for b in range(4)], axis=0)
    return out.astype(np.float32)
```
